# Optimizing a Trainium2 kernel written in Bass

```python
import math
import jax, jax.numpy as jnp
from jax import lax
import numpy as np

D_MODEL = 1024
BATCH = 16
SEQ = 2048
DEPTH = 1

N_META = 16
Q_BLOCK = 128
N_DIFF_HEADS = 4
DIFF_QK_DIM = 64
DIFF_V_DIM = 2 * DIFF_QK_DIM
DIFF_WIDTH = N_DIFF_HEADS * DIFF_V_DIM
N_SB_HEADS = 8
SB_HEAD_DIM = 64
SB_WIDTH = N_SB_HEADS * SB_HEAD_DIM
MIX_WIDTH = DIFF_WIDTH + SB_WIDTH
PROJ_SIZES = (
    N_DIFF_HEADS * 2 * DIFF_QK_DIM,
    N_DIFF_HEADS * 2 * DIFF_QK_DIM,
    N_DIFF_HEADS * DIFF_V_DIM,
    N_SB_HEADS * SB_HEAD_DIM,
    N_SB_HEADS * SB_HEAD_DIM,
    N_SB_HEADS * SB_HEAD_DIM,
)
PROJ_WIDTH = sum(PROJ_SIZES)
D_FF = 2816
CONV_WIDTH = 3
EPS = 1e-6

kernel_name = "hymba_diffattn_stickbreaking_convffn"


def _rmsnorm(x, g):
    xf = x.astype(jnp.float32)
    xf = xf * lax.rsqrt(jnp.mean(xf * xf, axis=-1, keepdims=True) + EPS)
    return xf.astype(x.dtype) * g


def _alibi_slopes(n_heads):
    start = 2.0 ** (-8.0 / n_heads)
    return jnp.asarray([start ** (i + 1) for i in range(n_heads)], dtype=jnp.float32)


def _lambda_init(layer_idx):
    return 0.8 - 0.6 * math.exp(-0.3 * layer_idx)


def _attend_block(qd1, qd2, qs, qpos, kd1, kd2, vd, ks, vs, kpos, slopes, lam):
    dist = qpos[:, None] - kpos[None, :]
    causal = dist >= 0
    alibi = -slopes[:, None, None] * dist.astype(jnp.float32)[None]

    def softmax_map(q, k):
        s = jnp.einsum("bhtd,bhsd->bhts", q, k).astype(jnp.float32) * (DIFF_QK_DIM ** -0.5) + alibi
        return jax.nn.softmax(jnp.where(causal, s, -jnp.inf), axis=-1)

    a_diff = softmax_map(qd1, kd1) - lam * softmax_map(qd2, kd2)
    o_diff = jnp.einsum("bhts,bhse->bhte", a_diff.astype(vd.dtype), vd)
    strict = dist > 0
    z = jnp.einsum("bhtd,bhsd->bhts", qs, ks).astype(jnp.float32) * (SB_HEAD_DIM ** -0.5)
    sp = jnp.where(strict, jax.nn.softplus(z), 0.0)
    log_a = z - lax.cumsum(sp, axis=3, reverse=True)
    a_sb = jnp.exp(jnp.where(strict, log_a, -jnp.inf))
    o_sb = jnp.einsum("bhts,bhse->bhte", a_sb.astype(vs.dtype), vs)
    return o_diff, o_sb


def _token_mixer(xn, w_in, lam_q1, lam_k1, lam_q2, lam_k2, g_diff, g_sb, w_out, layer_idx):
    B, L, _ = xn.shape
    n_blocks = (L - N_META) // Q_BLOCK
    proj = xn @ w_in
    dq, dk, dv, sq, sk, sv = jnp.split(proj, list(np.cumsum(PROJ_SIZES)[:-1]), axis=-1)

    def heads(t, h, d):
        return t.reshape(B, L, h, d).transpose(0, 2, 1, 3)

    dq = dq.reshape(B, L, N_DIFF_HEADS, 2, DIFF_QK_DIM)
    dk = dk.reshape(B, L, N_DIFF_HEADS, 2, DIFF_QK_DIM)
    qd1, qd2 = dq[..., 0, :].transpose(0, 2, 1, 3), dq[..., 1, :].transpose(0, 2, 1, 3)
    kd1, kd2 = dk[..., 0, :].transpose(0, 2, 1, 3), dk[..., 1, :].transpose(0, 2, 1, 3)
    vd = heads(dv, N_DIFF_HEADS, DIFF_V_DIM)
    qs, ks, vs = heads(sq, N_SB_HEADS, SB_HEAD_DIM), heads(sk, N_SB_HEADS, SB_HEAD_DIM), heads(sv, N_SB_HEADS, SB_HEAD_DIM)

    lam_init = _lambda_init(layer_idx)
    lam = (jnp.exp(jnp.sum(lam_q1.astype(jnp.float32) * lam_k1.astype(jnp.float32)))
           - jnp.exp(jnp.sum(lam_q2.astype(jnp.float32) * lam_k2.astype(jnp.float32))) + lam_init)
    slopes = _alibi_slopes(N_DIFF_HEADS)
    pos = jnp.arange(L, dtype=jnp.int32)

    m = N_META
    od_meta, os_meta = _attend_block(qd1[:, :, :m], qd2[:, :, :m], qs[:, :, :m], pos[:m],
                                     kd1[:, :, :m], kd2[:, :, :m], vd[:, :, :m],
                                     ks[:, :, :m], vs[:, :, :m], pos[:m], slopes, lam)

    def to_blocks(t):
        h, d = t.shape[1], t.shape[3]
        return t[:, :, m:].reshape(B, h, n_blocks, Q_BLOCK, d).transpose(2, 0, 1, 3, 4)

    def from_blocks(t):
        return t.transpose(1, 2, 0, 3, 4).reshape(B, t.shape[2], n_blocks * Q_BLOCK, t.shape[4])

    qpos_blocks = pos[m:].reshape(n_blocks, Q_BLOCK)
    od_real, os_real = lax.map(
        lambda a: _attend_block(a[0], a[1], a[2], a[3], kd1, kd2, vd, ks, vs, pos, slopes, lam),
        (to_blocks(qd1), to_blocks(qd2), to_blocks(qs), qpos_blocks))
    o_diff = jnp.concatenate([od_meta, from_blocks(od_real)], axis=2)
    o_sb = jnp.concatenate([os_meta, from_blocks(os_real)], axis=2)

    o_diff = _rmsnorm(o_diff, g_diff) * (1.0 - lam_init)
    o_sb = _rmsnorm(o_sb, g_sb)
    merged = jnp.concatenate([o_diff.transpose(0, 2, 1, 3).reshape(B, L, DIFF_WIDTH),
                              o_sb.transpose(0, 2, 1, 3).reshape(B, L, SB_WIDTH)], axis=-1)
    return merged @ w_out


def _conv_ffn(xn, w_up, conv_w, conv_b, w_down):
    L = xn.shape[1]
    h = xn @ w_up
    hp = jnp.pad(h, ((0, 0), (CONV_WIDTH - 1, 0), (0, 0)))
    hc = conv_b + sum(conv_w[k] * hp[:, k:k + L] for k in range(CONV_WIDTH))
    gate, up = jnp.split(hc, 2, axis=-1)
    return (jax.nn.silu(gate) * up) @ w_down


def setup_inputs(seed: int = 0) -> dict:
    key = jax.random.key(seed)
    ks = jax.random.split(key, 16)
    f32 = jnp.float32
    nrm = lambda k, shape, s: jax.random.normal(k, shape, f32) * s
    return {
        "x": nrm(ks[0], (BATCH, SEQ, D_MODEL), 1.0),
        "meta_tokens": nrm(ks[1], (N_META, D_MODEL), 1.0),
        "g_attn": 1.0 + nrm(ks[2], (DEPTH, D_MODEL), 0.02),
        "w_in": nrm(ks[3], (DEPTH, D_MODEL, PROJ_WIDTH), D_MODEL ** -0.5),
        "lam_q1": nrm(ks[4], (DEPTH, DIFF_QK_DIM), 0.1),
        "lam_k1": nrm(ks[5], (DEPTH, DIFF_QK_DIM), 0.1),
        "lam_q2": nrm(ks[6], (DEPTH, DIFF_QK_DIM), 0.1),
        "lam_k2": nrm(ks[7], (DEPTH, DIFF_QK_DIM), 0.1),
        "g_diff": 1.0 + nrm(ks[8], (DEPTH, DIFF_V_DIM), 0.02),
        "g_sb": 1.0 + nrm(ks[9], (DEPTH, SB_HEAD_DIM), 0.02),
        "w_out": nrm(ks[10], (DEPTH, MIX_WIDTH, D_MODEL), MIX_WIDTH ** -0.5),
        "g_ffn": 1.0 + nrm(ks[11], (DEPTH, D_MODEL), 0.02),
        "w_up": nrm(ks[12], (DEPTH, D_MODEL, 2 * D_FF), D_MODEL ** -0.5),
        "conv_w": nrm(ks[13], (DEPTH, CONV_WIDTH, 2 * D_FF), CONV_WIDTH ** -0.5),
        "conv_b": nrm(ks[14], (DEPTH, 2 * D_FF), 0.01),
        "w_down": nrm(ks[15], (DEPTH, D_FF, D_MODEL), D_FF ** -0.5),
        "g_final": 1.0 + nrm(jax.random.fold_in(key, 99), (D_MODEL,), 0.02),
    }


def reference(x, meta_tokens, g_attn, w_in, lam_q1, lam_k1, lam_q2, lam_k2, g_diff, g_sb,
              w_out, g_ffn, w_up, conv_w, conv_b, w_down, g_final):
    B = x.shape[0]
    meta = jnp.broadcast_to(meta_tokens[None].astype(x.dtype), (B, N_META, D_MODEL))
    h = jnp.concatenate([meta, x], axis=1)
    for layer in range(DEPTH):
        h = h + _token_mixer(_rmsnorm(h, g_attn[layer]), w_in[layer], lam_q1[layer], lam_k1[layer],
                             lam_q2[layer], lam_k2[layer], g_diff[layer], g_sb[layer],
                             w_out[layer], layer)
        h = h + _conv_ffn(_rmsnorm(h, g_ffn[layer]), w_up[layer], conv_w[layer], conv_b[layer],
                          w_down[layer])
    return _rmsnorm(h, g_final)[:, N_META:]
```

```python
import numpy as np
import concourse.bass as bass
import concourse.mybir as mybir
from concourse.bass_utils import run_bass_kernel_spmd

F32 = mybir.dt.float32
BF16 = mybir.dt.bfloat16
I32 = mybir.dt.int32
AF = mybir.ActivationFunctionType
ALU = mybir.AluOpType
AX = mybir.AxisListType

D = 1024
DFF = 2816
NEG = -4000.0
EPS = 1e-6
LAM_INIT = 0.8 - 0.6
SLOPES = [0.25 ** (i + 1) for i in range(4)]
EPOCH = 12000
NDUMMY = 0
ACC2_ENG = "dve"


def I(name, *a, **kw):
    return (name, a, kw)


class Trk:
    __slots__ = ("w", "r", "const", "name")

    def __init__(self, name="", const=False):
        self.w = None
        self.r = {}
        self.const = const
        self.name = name


class DSem:
    def __init__(self, prog):
        self.id = len(prog.dsems)
        self.issued = 0
        prog.dsems.append(self)


class Queue:
    def __init__(self, name):
        self.name = name
        self.ops = []
        self.count = 0
        self.epoch = 0
        self.known = {}
        self.pending = []


class Prog:
    ENG = ("pe", "act", "dve", "pool", "sp")

    def __init__(self):
        self.q = {n: Queue(n) for n in self.ENG}
        self.dsems = []

    def _need(self, q, waits, ev, kind):
        if ev is None:
            return
        key, val, ds = ev
        if ds is not None:
            val = ds.issued * 16
        if key[0] == "E" and key[1] == q.name:
            if q.name == "pe" or kind == "war":
                return
        if q.known.get(key, 0) >= val:
            return
        q.known[key] = val
        waits.append((key, val))

    def op(self, eng, fn, reads=(), writes=(), ds=None):
        if isinstance(fn, tuple):
            fn = [fn]
        q = self.q[eng]
        waits = q.pending
        q.pending = []
        for t in reads:
            self._need(q, waits, t.w, "raw")
        for t in writes:
            self._need(q, waits, t.w, "waw")
            for k, (v, d) in t.r.items():
                self._need(q, waits, (k, v, d), "war")
        if ds is None:
            if q.count >= EPOCH:
                q.epoch += 1
                q.count = 0
            q.count += 1
            ev = (("E", q.name, q.epoch), q.count, None)
            inc = 1
        else:
            ds.issued += 1
            ev = (("D", ds.id), ds.issued * 16, ds)
            inc = 16
        for t in reads:
            if not t.const:
                t.r[ev[0]] = (ev[1], ev[2])
        for t in writes:
            t.w = ev
            t.r = {}
        q.ops.append((waits, fn, ev[0], inc))
        return ev

    def barrier(self):
        evs = []
        for q in self.q.values():
            if q.name == "sp":
                continue
            for ep in range(q.epoch + 1):
                cnt = q.count if ep == q.epoch else EPOCH
                if cnt > 0:
                    evs.append((("E", q.name, ep), cnt, None))
        for d in self.dsems:
            if d.issued:
                evs.append((("D", d.id), d.issued * 16, d))
        for q in self.q.values():
            for ev in evs:
                if ev[0][0] == "E" and ev[0][1] == q.name:
                    continue
                self._need(q, q.pending, ev, "raw")

    def emit(self, nc):
        keys = []
        for q in self.q.values():
            for (_, _, k, _) in q.ops:
                if k not in keys:
                    keys.append(k)
        sems = {}
        import contextlib
        with contextlib.ExitStack() as st:
            for i, k in enumerate(keys):
                sems[k] = st.enter_context(nc.semaphore("s%d" % i))
            block = st.enter_context(nc.Block())

            def run(e, q):
                for (waits, fn, k, inc) in q.ops:
                    for (wk, wv) in waits:
                        e.wait_ge(sems[wk], wv)
                    ins = None
                    for (nm, a, kw) in fn:
                        ins = getattr(e, nm)(*a, **kw)
                    ins.then_inc(sems[k], inc)
                for (wk, wv) in q.pending:
                    e.wait_ge(sems[wk], wv)

            @block.tensor
            def _(e):
                run(e, self.q["pe"])

            @block.scalar
            def _(e):
                run(e, self.q["act"])

            @block.vector
            def _(e):
                run(e, self.q["dve"])

            @block.gpsimd
            def _(e):
                run(e, self.q["pool"])

            @block.sync
            def _(e):
                run(e, self.q["sp"])


def build(nseq, SEQ, dbg=False):
    assert SEQ % 512 == 0
    L = SEQ + 16
    NT = SEQ // 512
    NBF = SEQ // 128
    NB = NBF + 1
    nc = bass.Bass("TRN2", target_bir_lowering=False)
    P = Prog()

    def din(name, shape):
        return nc.dram_tensor(name, shape, F32, kind="ExternalInput").ap()

    x_d = din("x", [nseq, SEQ, D])
    meta_d = din("meta", [16, D])
    w_in_d = din("w_in", [D, 3072])
    w_out_d = din("w_out", [D, D])
    w_up_d = din("w_up", [D, 2 * DFF])
    w_down_d = din("w_down", [DFF, D])
    convw_d = din("convw", [4, 2 * DFF])
    g_attn_d = din("g_attn", [1, D])
    g_ffn_d = din("g_ffn", [1, D])
    g_fin_d = din("g_fin", [1, D])
    g_diff_d = din("g_diff", [1, 128])
    g_sb_d = din("g_sb", [1, 64])
    lam_d = din("lam", [4, 64])
    out_d = nc.dram_tensor("out", [nseq, SEQ, D], F32, kind="ExternalOutput").ap()
    dbg_d = {}
    if dbg:
        dbg_d["qkT"] = nc.dram_tensor("dbg_qkT", [128, 16 * L], BF16, kind="ExternalOutput").ap()
        dbg_d["v"] = nc.dram_tensor("dbg_v", [128, NB * 1024], BF16, kind="ExternalOutput").ap()
        dbg_d["mg"] = nc.dram_tensor("dbg_mg", [128, 8 * L], BF16, kind="ExternalOutput").ap()
        dbg_d["h1"] = nc.dram_tensor("dbg_h1", [128, NB * 1024], F32, kind="ExternalOutput").ap()

    import contextlib
    st = contextlib.ExitStack()

    def sb(name, shape, dt):
        return st.enter_context(nc.sbuf_tensor(name, shape, dt))

    W_X = max(8 * L // 2, 8256)
    W_R = max(16 * L // 2 + NB * 512, NB * 1024 + 4 * L // 2 + 2048 + 640, NB * 1024 + 4096)
    W_E = 6144
    arena = sb("arena", [128, W_R + 2 * W_X + W_E], F32)[:]
    O_R, O_X, O_M, O_E = 0, W_R, W_R + W_X, W_R + 2 * W_X

    def view(off, shape, dt):
        n = int(np.prod(shape))
        nw = n if dt == F32 else n // 2
        ap = arena[:, off:off + nw]
        if dt != F32:
            ap = ap.bitcast(dt)
        return ap.rearrange("p (a b) -> p a b", a=shape[0])

    def flat(off, nwords, dt=F32):
        ap = arena[:, off:off + nwords]
        return ap if dt == F32 else ap.bitcast(dt)

    qkT = view(O_R, [16, L], BF16)
    vtk = view(O_R + 16 * L // 2, [NB, 1024], BF16)
    h1 = view(O_R, [NB, 1024], F32)
    a_g = view(O_R + NB * 1024, [4, L], BF16)
    wd0 = view(O_R + NB * 1024 + 4 * L // 2, [4, 1024], BF16)
    O_RJ = O_R + NB * 1024 + 4 * L // 2 + 2048
    wout = view(O_R + NB * 1024, [8, 1024], BF16)
    bufX = view(O_X, [8, L], BF16)
    bufM = view(O_M, [8, L], BF16)

    identb = sb("identb", [128, 128], BF16)[:]
    identf = sb("identf", [128, 128], F32)[:]
    trib = sb("trib", [128, 128], BF16)[:]
    onesb = sb("onesb", [128, 128], BF16)[:]
    zerosb = sb("zerosb", [128, 128], BF16)[:]
    maskD = sb("maskD", [128, 128], BF16)[:]
    maskS = sb("maskS", [128, 128], BF16)[:]
    maskSp = sb("maskSp", [128, 128], BF16)[:]
    tmpf = sb("tmpf", [128, 128], F32)[:]
    biasT = sb("biasT", [128, 4, 25], F32)[:]
    metab = sb("metab", [128, 1], F32)[:]
    iot = sb("iot", [128, 25], I32)[:]
    iotf = sb("iotf", [128, 25], F32)[:]
    cw = sb("cw", [128, 4, 44], F32)[:]
    cwrow = sb("cwrow", [44, 4, 128], F32)[:]
    cols = sb("cols", [128, 8], F32)[:]
    rows = sb("rows", [1, 640], F32)[:]
    onesrow = sb("onesrow", [1, 128], F32)[:]
    onesf = sb("onesf", [128, 128], F32)[:]
    onesrowb = sb("onesrowb", [1, 128], BF16)[:]
    qb8 = sb("qb8", [1, 512], BF16)[:]
    onesr0 = sb("onesr0", [128, 128], BF16)[:]
    qb8p = sb("qb8p", [128, 512], BF16)[:]
    qb8i = sb("qb8i", [1, 512], I32)[:]
    qb8f = sb("qb8f", [1, 512], F32)[:]

    banks = [st.enter_context(nc.psum_tensor("bank%d" % i, [128, 512], F32))[:] for i in range(8)]
    bankT = [Trk("bank%d" % i) for i in range(8)]

    T_const = Trk("const")
    sem_setup = DSem(P)

    def pl(ins):
        P.op("pool", ins, writes=(T_const,))

    pl(I("memset", tmpf, 1.0))
    pl(I("affine_select", identf, tmpf, [[-1, 128]], ALU.is_equal, 0.0, base=0, channel_multiplier=1))
    pl(I("tensor_copy", identb, identf))
    pl(I("affine_select", identf, tmpf, [[-1, 128]], ALU.is_ge, 0.0, base=0, channel_multiplier=1))
    pl(I("tensor_copy", trib, identf))
    pl(I("memset", onesb, 1.0))
    pl(I("memset", zerosb, 0.0))
    pl(I("memset", onesrow, 1.0))
    pl(I("memset", onesf, 1.0))
    pl(I("memset", onesrowb, 1.0))
    pl(I("iota", qb8i, [[-2, 512]], base=512, channel_multiplier=0))
    pl(I("tensor_copy", qb8f, qb8i))
    pl(I("tensor_copy", qb8, qb8f))
    pl(I("memset", onesr0, 0.0))
    pl(I("memset", onesr0[0:1, :], 1.0))
    pl(I("memset", qb8p, 0.0))
    pl(I("tensor_copy", qb8p[0:1, :], qb8f))
    pl(I("memset", tmpf, 0.0))
    pl(I("affine_select", identf, tmpf, [[1, 128]], ALU.is_ge, NEG, base=0, channel_multiplier=-1))
    pl(I("tensor_copy", maskD, identf))
    pl(I("affine_select", identf, tmpf, [[1, 128]], ALU.is_gt, NEG, base=0, channel_multiplier=-1))
    pl(I("tensor_copy", maskS, identf))
    pl(I("affine_select", identf, tmpf, [[1, 128]], ALU.is_gt, -NEG, base=0, channel_multiplier=-1))
    pl(I("tensor_copy", maskSp, identf))
    pl(I("memset", tmpf, 1.0))
    pl(I("affine_select", identf, tmpf, [[-1, 128]], ALU.is_equal, 0.0, base=0, channel_multiplier=1))
    pl(I("iota", iot[:, 0:20], [[128, 20]], base=-16 * 128 - 256, channel_multiplier=1))
    pl(I("iota", iot[:, 20:24], [[-512, 4]], base=-272, channel_multiplier=1))
    pl(I("tensor_copy", iotf[:, 0:24], iot[:, 0:24]))
    pl(I("tensor_copy", iotf[:, 24:25], iotf[:, 16:17]))
    for h in range(4):
        pl(I("tensor_scalar", biasT[:, h, :], iotf, float(SLOPES[h]), None, ALU.mult))
    pl(I("tensor_scalar", metab, iotf[:, 16:17], -240.5, -30000.0, ALU.is_gt, ALU.mult))
    for h in range(4):
        pl(I("tensor_scalar", biasT[:, h, 20:25], biasT[:, h, 20:25], metab[:, 0:1], None, ALU.add))
    T_ld = Trk("ld")

    def ld(out, in_):
        P.op("sp", I("dma_start", out=out, in_=in_), writes=(T_ld,), ds=sem_setup)

    for k in range(4):
        ld(cwrow[:, k, :], convw_d[k].rearrange("(c p) -> c p", p=128))
    ld(rows[:, 0:128], g_diff_d)
    ld(rows[:, 128:192], g_sb_d)
    ld(rows[:, 192:256], g_sb_d)
    ld(rows[:, 256:512], lam_d.rearrange("(o a) b -> o (a b)", o=1))
    for k in range(4):
        P.op("pe", I("matmul", banks[0][:, 64 * k:64 * k + 44], lhsT=cwrow[:, k, :], rhs=identf[0:44, 0:44],
                     start=True, stop=True), reads=(T_ld, T_const), writes=(bankT[0],))
    for k in range(4):
        P.op("dve", I("tensor_copy", cw[:, k, :], banks[0][:, 64 * k:64 * k + 44]),
             reads=(bankT[0],), writes=(T_const,))
    P.op("pe", I("matmul", banks[1][:, 0:1], lhsT=rows[0:1, 0:128], rhs=identf[0:1, 0:1], start=True, stop=True),
         reads=(T_ld, T_const), writes=(bankT[1],))
    P.op("pe", I("matmul", banks[1][:, 1:2], lhsT=rows[0:1, 128:256], rhs=identf[0:1, 0:1], start=True, stop=True),
         reads=(T_ld, T_const), writes=(bankT[1],))
    T_l = Trk("lam")
    P.op("dve", I("tensor_tensor", rows[:, 512:576], rows[:, 256:320], rows[:, 320:384], ALU.mult),
         reads=(T_ld,), writes=(T_l,))
    P.op("dve", I("tensor_tensor", rows[:, 576:640], rows[:, 384:448], rows[:, 448:512], ALU.mult),
         reads=(T_l, T_ld), writes=(T_l,))
    P.op("dve", I("reduce_sum", rows[:, 256:258], rows[:, 512:640].rearrange("o (a b) -> o a b", a=2), AX.X),
         reads=(T_l,), writes=(T_l,))
    P.op("act", I("activation", rows[:, 260:262], rows[:, 256:258], AF.Exp), reads=(T_l,), writes=(T_l,))
    P.op("dve", I("tensor_tensor", rows[:, 264:265], rows[:, 261:262], rows[:, 260:261], ALU.subtract),
         reads=(T_l,), writes=(T_l,))
    P.op("dve", I("tensor_scalar", rows[:, 266:267], rows[:, 264:265], -LAM_INIT, None, ALU.add),
         reads=(T_l,), writes=(T_l,))
    P.op("pe", I("matmul", banks[1][:, 2:3], lhsT=onesrow[0:1, :], rhs=rows[0:1, 266:267], start=True, stop=True),
         reads=(T_l, T_const), writes=(bankT[1],))
    P.op("dve", I("tensor_copy", cols[:, 0:3], banks[1][:, 0:3]), reads=(bankT[1],), writes=(T_const,))
    P.op("dve", I("tensor_scalar", cols[:, 0:1], cols[:, 0:1], 1.0 - LAM_INIT, None, ALU.mult),
         reads=(T_const,), writes=(T_const,))
    P.barrier()
    T_const.const = True
    gdc = cols[:, 0:1]
    gsbc = cols[:, 1:2]
    nlamc = cols[:, 2:3]

    sem_g = DSem(P)

    def load_bc(dst, src, trk):
        P.op("sp", I("dma_start", out=dst, in_=bass.AP(src.tensor, 0, [[0, 128], [1, D]])), writes=(trk,), ds=sem_g)

    def blk(tb):
        return (0, 16) if tb == 0 else (16 + 128 * (tb - 1), 128)

    def blocks_of(p0, n):
        return [tb for tb in range(NB) if blk(tb)[0] < p0 + n and blk(tb)[0] + blk(tb)[1] > p0]

    def tile_of_block(tb):
        return 0 if tb == 0 else 1 + (tb - 1) // 4

    def rstd_ops(ss, rs, n, trk_ss, trk_rs, scale):
        P.op("act", I("activation", rs[0:n], ss[0:n], AF.Ln, bias=EPS, scale=scale),
             reads=(trk_ss,), writes=(trk_rs,))
        P.op("act", I("activation", rs[0:n], rs[0:n], AF.Exp, scale=-0.5), reads=(trk_rs,), writes=(trk_rs,))

    sem_x = [DSem(P) for _ in range(3)]
    sem_w = [DSem(P) for _ in range(4)]
    sem_o = DSem(P)
    evac_rr = [0]

    def evac(out, in_, reads, writes):
        evac_rr[0] ^= 1
        if evac_rr[0]:
            P.op("act", I("activation", out, in_, AF.Copy), reads=reads, writes=writes)
        else:
            P.op("dve", I("tensor_copy", out, in_), reads=reads, writes=writes)

    def norm_transpose(src_rows, src_trk, n, tb, g_bc, T_g, xs, T_xs, ss, T_ss, rs, T_rs, junk, T_junk,
                       pbank, dst, T_dst):
        p0 = blk(tb)[0]
        P.op("act", I("activation", junk[0:n], src_rows, AF.Square, accum_out=ss[0:n]),
             reads=(src_trk,), writes=(T_junk, T_ss))
        rstd_ops(ss, rs, n, T_ss, T_rs, 1.0 / D)
        P.op("dve", I("scalar_tensor_tensor", xs[0:n], src_rows, rs[0:n], g_bc[0:n], ALU.mult, ALU.mult),
             reads=(src_trk, T_rs, T_g), writes=(T_xs,))
        pT = banks[pbank].bitcast(BF16).rearrange("p (c t) -> p c t", c=8)
        P.op("pe", [I("transpose", pT[:, c, 0:n], xs[0:n, 128 * c:128 * (c + 1)], identb[0:n, 0:n])
                    for c in range(8)], reads=(T_xs, T_const), writes=(bankT[pbank],))
        return lambda: evac(dst[:, :, p0:p0 + n], pT[:, :, 0:n], (bankT[pbank],), (T_dst,))

    def load_x(s, tb, dst, trk, ds):
        if tb == 0:
            P.op("sp", I("dma_start", out=dst[0:16], in_=meta_d), writes=(trk,), ds=ds)
        else:
            P.op("sp", I("dma_start", out=dst[0:128], in_=x_d[s, 128 * (tb - 1):128 * tb, :]), writes=(trk,), ds=ds)

    def keyblocks(qt):
        return [0] if qt == 0 else list(range(4 * qt + 1))

    def geom(qt, kb):
        kp, nk = blk(kb)
        nk = 128
        if qt == 0:
            return 16, nk, kp, 0, True, 16, 24
        if kb == 0:
            return 512, nk, kp, 0, False, 128, 19 + qt
        rel = (kb - 1) - 4 * (qt - 1)
        return 512, nk, kp, (128 * rel if rel > 0 else 0), rel >= 0, 128, rel + 16

    def qtile(qt):
        return (0, 16) if qt == 0 else (16 + 512 * (qt - 1), 512)

    w_in_v = w_in_d.rearrange("(cc p) f -> p cc f", p=128)
    w_out_v = w_out_d.rearrange("(cc p) f -> p cc f", p=128)
    w_up_v = w_up_d.rearrange("(cc p) f -> p cc f", p=128)
    w_dn_v = w_down_d.rearrange("(fc p) d -> p fc d", p=128)
    ttiles = [qtile(qt) for qt in range(NT + 1)]

    for s in range(nseq):
        oM = O_M
        xin = [flat(oM + 1024 * i, 1024) for i in range(3)]
        T_xin = [Trk() for _ in range(3)]
        xs2 = [flat(oM + 3072 + 512 * i, 512, BF16) for i in range(2)]
        T_xs2 = [Trk() for _ in range(2)]
        junk = flat(oM + 4096, 512, BF16)
        T_junk = Trk()
        wsl = [flat(oM + 4608 + 1024 * i, 1024, BF16).rearrange("p (c f) -> p c f", c=8) for i in range(3)]
        T_wsl = [Trk() for _ in range(3)]
        ssb = flat(O_E + 4096, 64)
        T_ss = [Trk() for _ in range(4)]
        T_rs = [Trk() for _ in range(4)]
        g_bc = flat(O_E, 1024)
        T_g = Trk()
        load_bc(g_bc, g_attn_d, T_g)
        T_xn = [Trk() for _ in range(NB)]
        T_qk = [[Trk() for _ in range(NT + 1)] for _ in range(16)]
        T_v = [Trk() for _ in range(NB)]
        P.op("pool", I("memset", vtk[:, 0, :], 0.0), writes=(T_v[0],))
        prev_fin = None
        for tb in range(NB):
            p0, n = blk(tb)
            i3, i2, k4 = tb % 3, tb % 2, tb % 4
            load_x(s, tb, xin[i3], T_xin[i3], sem_x[i3])
            fin = norm_transpose(xin[i3][0:n], T_xin[i3], n, tb, g_bc, T_g, xs2[i2], T_xs2[i2],
                                 ssb[:, k4:k4 + 1], T_ss[k4], ssb[:, 8 + k4:9 + k4], T_rs[k4], junk, T_junk,
                                 6 + i2, bufX, T_xn[tb])
            if prev_fin is not None:
                prev_fin()
            prev_fin = fin
        prev_fin()
        pb = 0
        for sl in range(12):
            i3 = sl % 3
            P.op("pool", I("dma_start", out=wsl[i3], in_=w_in_v[:, :, 256 * sl:256 * sl + 256]),
                 writes=(T_wsl[i3],), ds=sem_w[i3])
            c0 = 2 * sl
            if c0 < 8 or 12 <= c0 < 20:
                for fi in range(2):
                    c = c0 + fi
                    qc = c if c < 8 else c - 4
                    for ti, (t0, n) in enumerate(ttiles):
                        b = pb % 6
                        pb += 1
                        blks = blocks_of(t0, n)
                        P.op("pe", [I("matmul", banks[b][:, 0:n], lhsT=wsl[i3][:, cc, 128 * fi:128 * fi + 128],
                                      rhs=bufX[:, cc, t0:t0 + n], start=(cc == 0), stop=(cc == 7))
                                    for cc in range(8)],
                             reads=[T_wsl[i3]] + [T_xn[tb] for tb in blks], writes=(bankT[b],))
                        if 8 <= qc < 12:
                            P.op("dve", I("tensor_scalar", qkT[:, qc, t0:t0 + n], banks[b][:, 0:n], -0.125, None,
                                          ALU.mult), reads=(bankT[b],), writes=(T_qk[qc][ti],))
                        else:
                            evac(qkT[:, qc, t0:t0 + n], banks[b][:, 0:n], (bankT[b],), (T_qk[qc][ti],))
            else:
                vcol = 256 * (sl - 4) if sl < 6 else 512 + 256 * (sl - 10)
                for tb in range(NB):
                    p0, n = blk(tb)
                    b = pb % 6
                    pb += 1
                    P.op("pe", [I("matmul", banks[b][0:n, 0:256], lhsT=bufX[:, cc, p0:p0 + n], rhs=wsl[i3][:, cc, :],
                                  start=(cc == 0), stop=(cc == 7)) for cc in range(8)],
                         reads=(T_wsl[i3], T_xn[tb]), writes=(bankT[b],))
                    evac(vtk[0:n, tb, vcol:vcol + 256], banks[b][0:n, 0:256], (bankT[b],), (T_v[tb],))
        P.barrier()
        if dbg and s == 0:
            dsd = DSem(P)
            P.op("sp", I("dma_start", out=dbg_d["qkT"], in_=flat(O_R, 16 * L // 2, BF16)), ds=dsd)
            P.op("sp", I("dma_start", out=dbg_d["v"], in_=flat(O_R + 16 * L // 2, NB * 512, BF16)), ds=dsd)
            P.barrier()

        T_mg = [[Trk() for _ in range(NT + 1)] for _ in range(8)]
        o = O_X
        Pt = [flat(o + 256 * i, 256, BF16) for i in range(4)]
        T_P = [Trk() for _ in range(4)]
        o += 1024
        Et = [flat(o + 512 * i, 512) for i in range(2)]
        T_E = [Trk() for _ in range(2)]
        o += 1024
        SPt = [flat(o + 256 * i, 256, BF16) for i in range(3)]
        T_SP = [Trk() for _ in range(3)]
        o += 768
        QP = [[flat(o + 512 * i + 256 * j, 256, BF16) for j in range(2)] for i in range(2)]
        T_QP = [[Trk() for j in range(2)] for i in range(2)]
        o += 1024
        At = [flat(o + 256 * i, 256, BF16) for i in range(3)]
        T_A = [Trk() for _ in range(3)]
        o += 768
        RS32 = flat(o, 512)
        T_RS32 = Trk()
        o += 512
        RSb = [flat(o + 256 * i, 256, BF16) for i in range(2)]
        T_RSb = [Trk() for _ in range(2)]
        o += 512
        Pacc = [[flat(o + 1024 * i + 512 * j, 512) for j in range(2)] for i in range(2)]
        T_Pacc = [[Trk() for j in range(2)] for i in range(2)]
        o += 2048
        assert o - O_X <= W_X, (o - O_X, W_X)
        ep = [flat(O_E + 512 * i, 512) for i in range(5)]
        T_ep = [Trk() for _ in range(5)]
        sqb = flat(O_E + 2560, 256, BF16)
        T_sq = Trk()
        for i_ in range(2):
            for j_ in range(2):
                P.op("pool", I("memset", QP[i_][j_], 0.0), writes=(T_QP[i_][j_],))

        om = O_E + 2816
        QPm = flat(om, 128, BF16).rearrange("p (s q) -> p s q", s=16)
        Pm = flat(om + 128, 64, BF16)
        Em = flat(om + 192, 128)
        SPm = flat(om + 320, 64, BF16)
        Am = flat(om + 384, 64, BF16)
        T_m = Trk()
        P.op("dve", I("memset", QPm, 0.0), writes=(T_m,))
        for m in range(2):
            P.op("dve", I("tensor_copy", QPm[64 * m:64 * m + 64, m:8:2, :], qkT[64 * m:64 * m + 64, 0:4, 0:16]),
                 reads=[T_qk[c][0] for c in range(4)], writes=(T_m,))
            P.op("dve", I("tensor_copy", QPm[64 * m:64 * m + 64, 8 + 4 * m:12 + 4 * m, :],
                          qkT[64 * m:64 * m + 64, 8:12, 0:16]), reads=[T_qk[c][0] for c in range(8, 12)],
                 writes=(T_m,))
        for h in range(4):
            for m in range(2):
                cs_ = 16 * (2 * h + m)
                ins = [I("matmul", banks[0][:, cs_:cs_ + 16], lhsT=qkT[:, 4 + h, 0:128], rhs=QPm[:, 2 * h + m, :],
                         start=True, stop=False)]
                if h == 0:
                    ins.append(I("matmul", banks[0][:, cs_:cs_ + 16], lhsT=onesr0, rhs=qb8p[:, 0:16], start=False,
                                 stop=False))
                ins.append(I("matmul", banks[0][:, cs_:cs_ + 16], lhsT=identb, rhs=maskD[:, 0:16], start=False,
                             stop=True))
                P.op("pe", ins, reads=(T_qk[4 + h][0], T_m, T_const), writes=(bankT[0],))
        T_Pm = Trk()
        for h in range(4):
            P.op("act", I("activation", Pm[:, 32 * h:32 * h + 32], banks[0][:, 32 * h:32 * h + 32], AF.Exp,
                          bias=biasT[:, h, 24:25], scale=0.125), reads=(bankT[0], T_const), writes=(T_Pm,))
        P.op("pe", I("matmul", banks[7][:, 0:128], lhsT=onesb, rhs=Pm, start=True, stop=True),
             reads=(T_Pm, T_const), writes=(bankT[7],))
        for h in range(4):
            for m in range(2):
                cs_ = 16 * (2 * h + m)
                P.op("pe", I("matmul", banks[2][:, cs_:cs_ + 16], lhsT=vtk[:, 0, 128 * h:128 * h + 128],
                             rhs=Pm[:, cs_:cs_ + 16], start=True, stop=True), reads=(T_Pm, T_v[0]),
                     writes=(bankT[2],))
        e0, e1, e2, e3, e4 = [t[:, 0:128] for t in ep]
        P.op("act", I("activation", e0, banks[7][:, 0:128], AF.Ln), reads=(bankT[7],), writes=(T_ep[0],))
        P.op("act", I("activation", e0, e0, AF.Exp, scale=-1.0), reads=(T_ep[0],), writes=(T_ep[0],))
        P.op("dve", I("tensor_tensor", e1, banks[2][:, 0:128], e0, ALU.mult), reads=(bankT[2], T_ep[0]),
             writes=(T_ep[1],))
        e1v = e1.rearrange("p (h m q) -> p h m q", h=4, m=2)
        e3v = ep[3][:, 0:64].rearrange("p (h q) -> p h q", h=4)
        e4v = ep[4][:, 0:64].rearrange("p (h q) -> p h q", h=4)
        P.op("dve", I("scalar_tensor_tensor", e3v, e1v[:, :, 1, :], nlamc, e1v[:, :, 0, :], ALU.mult, ALU.add),
             reads=(T_ep[1], T_const), writes=(T_ep[3],))
        P.op("dve", I("tensor_tensor", sqb[:, 0:64], ep[3][:, 0:64], ep[3][:, 0:64], ALU.mult), reads=(T_ep[3],),
             writes=(T_sq,))
        P.op("pe", I("matmul", banks[7][:, 0:64], lhsT=onesb, rhs=sqb[:, 0:64], start=True, stop=True),
             reads=(T_sq, T_const), writes=(bankT[7],))
        P.op("act", I("activation", ep[4][:, 0:64], banks[7][:, 0:64], AF.Ln, bias=EPS, scale=1.0 / 128),
             reads=(bankT[7],), writes=(T_ep[4],))
        P.op("act", I("activation", ep[4][:, 0:64], ep[4][:, 0:64], AF.Exp, scale=-0.5), reads=(T_ep[4],),
             writes=(T_ep[4],))
        P.op("dve", I("scalar_tensor_tensor", bufM[:, 0:4, 0:16], e3v, gdc, e4v, ALU.mult, ALU.mult),
             reads=(T_ep[3], T_ep[4], T_const), writes=[T_mg[h][0] for h in range(4)])
        def cg_(g):
            return 16 * (4 * (g % 2) + g // 2)

        def zt_m(bank, g, stop):
            return [I("matmul", bank[:, cg_(g):cg_(g) + 16], lhsT=qkT[:, 12 + g // 2, 0:128],
                      rhs=QPm[:, 8 + 4 * (g % 2) + g // 2, :], start=True, stop=False),
                    I("matmul", bank[:, cg_(g):cg_(g) + 16], lhsT=identb, rhs=maskSp[:, 0:16], start=False,
                      stop=stop)]
        for g in range(8):
            P.op("pe", zt_m(banks[0], g, True), reads=(T_qk[12 + g // 2][0], T_m, T_const), writes=(bankT[0],))
        T_Em = Trk()
        T_SPm = Trk()
        T_Am = Trk()
        P.op("act", I("activation", Em, banks[0][:, 0:128], AF.Exp, scale=-1.0, bias=metab),
             reads=(bankT[0], T_const), writes=(T_Em,))
        P.op("act", I("activation", SPm, Em, AF.Ln, bias=1.0, scale=1.0), reads=(T_Em,), writes=(T_SPm,))
        for g in range(8):
            P.op("pe", zt_m(banks[1], g, False) +
                 [I("matmul", banks[1][:, cg_(g):cg_(g) + 16], lhsT=trib, rhs=SPm[:, cg_(g):cg_(g) + 16],
                    start=False, stop=True)],
                 reads=(T_qk[12 + g // 2][0], T_m, T_SPm, T_const), writes=(bankT[1],))
        P.op("act", I("activation", Am, banks[1][:, 0:128], AF.Exp, scale=-1.0, bias=metab),
             reads=(bankT[1], T_const), writes=(T_Am,))
        for g in range(8):
            vc = 512 + 128 * (g // 2)
            P.op("pe", I("matmul", banks[3][:, cg_(g):cg_(g) + 16], lhsT=vtk[:, 0, vc:vc + 128],
                         rhs=Am[:, cg_(g):cg_(g) + 16], start=True, stop=True), reads=(T_Am, T_v[0]),
                 writes=(bankT[3],))
        for p_ in range(2):
            rr = slice(64 * p_, 64 * p_ + 64)
            cc_ = slice(64 * p_, 64 * p_ + 64)
            P.op("dve", I("tensor_copy", ep[1][rr, cc_], banks[3][rr, cc_]), reads=(bankT[3],), writes=(T_ep[1],))
            P.op("dve", I("tensor_tensor", sqb[rr, cc_], ep[1][rr, cc_], ep[1][rr, cc_], ALU.mult),
                 reads=(T_ep[1],), writes=(T_sq,))
            P.op("pe", I("matmul", banks[7][:, cc_], lhsT=onesb[rr, :], rhs=sqb[rr, cc_], start=True, stop=True),
                 reads=(T_sq, T_const), writes=(bankT[7],))
            P.op("act", I("activation", ep[4][rr, cc_], banks[7][rr, cc_], AF.Ln, bias=EPS, scale=1.0 / 64),
                 reads=(bankT[7],), writes=(T_ep[4],))
            P.op("act", I("activation", ep[4][rr, cc_], ep[4][rr, cc_], AF.Exp, scale=-0.5), reads=(T_ep[4],),
                 writes=(T_ep[4],))
            P.op("dve", I("scalar_tensor_tensor", bufM[rr, 4:8, 0:16],
                          ep[1][rr, cc_].rearrange("p (h q) -> p h q", h=4), gsbc[rr],
                          ep[4][rr, cc_].rearrange("p (h q) -> p h q", h=4), ALU.mult, ALU.mult),
                 reads=(T_ep[1], T_ep[4], T_const), writes=[T_mg[4 + c][0] for c in range(4)])

        tasks = [("d", h, qt) for qt in range(NT, 0, -1) for h in range(4)] + \
                [("s", g, qt) for qt in range(NT, 0, -1) for g in range(8)]
        pending = []
        ctr = {"S": 0, "P": 0, "Z": 0, "E": 0, "SP": 0, "A": 0, "R": 0, "CS": 0}

        def run_pending(k):
            for _ in range(k):
                if pending:
                    pending.pop(0)()

        def prep(ti):
            kind, hh, qt = tasks[ti]
            q0, nq = qtile(qt)
            st_ = ti % 2
            if kind == "d":
                for m in range(2):
                    P.op("pool", I("tensor_copy", QP[st_][m][64 * m:64 * m + 64, 0:nq],
                                   qkT[64 * m:64 * m + 64, hh, q0:q0 + nq]), reads=(T_qk[hh][qt],),
                         writes=(T_QP[st_][m],))
            else:
                r0 = 64 * (hh % 2)
                P.op("dve", I("tensor_copy", QP[st_][hh % 2][r0:r0 + 64, 0:nq],
                              qkT[r0:r0 + 64, 8 + hh // 2, q0:q0 + nq]), reads=(T_qk[8 + hh // 2][qt],),
                     writes=(T_QP[st_][hh % 2],))

        ctx = []
        for ti, (kind, hh, qt) in enumerate(tasks):
            q0, nq = qtile(qt)
            kbs = keyblocks(qt)
            if kind == "d":
                if qt > 0:
                    kbs = kbs[1:] + [0]
            else:
                kbs = kbs[::-1]
            ctx.append(dict(ti=ti, kind=kind, hh=hh, qt=qt, q0=q0, nq=nq, kbs=kbs, nkb=len(kbs), st=ti % 2))
        n_d = 4 * NT
        d_items = [(ti, i) for ti in range(n_d) for i in range(ctx[ti]["nkb"])]
        s_items = [(ti, i) for ti in range(n_d, len(tasks)) for i in range(ctx[ti]["nkb"])]

        sb_of = {}

        def d_S(item, m):
            ti, i = item
            c = ctx[ti]
            h, qt, nq, st_ = c["hh"], c["qt"], c["nq"], c["st"]
            kb = c["kbs"][i]
            _, nk, kp, c0, diag, mw, bi = geom(qt, kb)
            sbk = [0, 1, 2][ctr["S"] % 3]
            ctr["S"] += 1
            ins = [I("matmul", banks[sbk][0:nk, c0:nq], lhsT=qkT[:, 4 + h, kp:kp + nk],
                     rhs=QP[st_][m][:, c0:nq], start=True, stop=(not diag and h != 0))]
            if h == 0:
                ins.append(I("matmul", banks[sbk][0:nk, c0:nq], lhsT=onesr0[:, 0:nk],
                             rhs=qb8p[:, c0:nq], start=False, stop=not diag))
            if diag:
                ins.append(I("matmul", banks[sbk][0:nk, c0:c0 + mw], lhsT=identb[0:nk, 0:nk],
                             rhs=maskD[0:nk, 0:mw], start=False, stop=True))
            P.op("pe", ins, reads=(T_qk[4 + h][tile_of_block(kb)], T_QP[st_][m], T_const), writes=(bankT[sbk],))
            sb_of[(item, m)] = sbk

        def d_epi(c):
            h, qt, q0, nq, st_ = c["hh"], c["qt"], c["q0"], c["nq"], c["st"]
            OBK = [3, 4] if st_ == 0 else [5, 6]
            O1, O2 = [banks[a][:, 0:nq] for a in OBK]
            S7 = banks[7][:, 0:nq]
            e0, e1, e2, e3, e4 = [t[:, 0:nq] for t in ep]

            def t0():
                P.op("pe", I("matmul", S7, lhsT=onesf, rhs=Pacc[st_][0][:, 0:nq], start=True, stop=True),
                     reads=(T_Pacc[st_][0], T_const), writes=(bankT[7],))

            def t1():
                P.op("act", I("activation", e0, S7, AF.Ln), reads=(bankT[7],), writes=(T_ep[0],))

            def t2():
                P.op("pe", I("matmul", S7, lhsT=onesf, rhs=Pacc[st_][1][:, 0:nq], start=True, stop=True),
                     reads=(T_Pacc[st_][1], T_const), writes=(bankT[7],))

            def t3():
                P.op("act", I("activation", e2, S7, AF.Ln), reads=(bankT[7],), writes=(T_ep[2],))
                P.op("act", I("activation", e0, e0, AF.Exp, scale=-1.0), reads=(T_ep[0],), writes=(T_ep[0],))

            def t4():
                P.op("act", I("activation", e2, e2, AF.Exp, scale=-1.0), reads=(T_ep[2],), writes=(T_ep[2],))
                P.op("dve", I("tensor_tensor", e1, O1, e0, ALU.mult), reads=(bankT[OBK[0]], T_ep[0]),
                     writes=(T_ep[1],))

            def t5():
                P.op("dve", I("tensor_tensor", e3, O2, e2, ALU.mult), reads=(bankT[OBK[1]], T_ep[2]),
                     writes=(T_ep[3],))

            def t6():
                P.op("dve", I("scalar_tensor_tensor", e1, e3, nlamc, e1, ALU.mult, ALU.add),
                     reads=(T_ep[3], T_ep[1], T_const), writes=(T_ep[1],))

            def t7():
                P.op("pool", I("tensor_tensor", sqb[:, 0:nq], e1, e1, ALU.mult), reads=(T_ep[1],), writes=(T_sq,))

            def t8():
                P.op("pe", I("matmul", S7, lhsT=onesb, rhs=sqb[:, 0:nq], start=True, stop=True),
                     reads=(T_sq, T_const), writes=(bankT[7],))

            def t9():
                P.op("act", I("activation", e4, S7, AF.Ln, bias=EPS, scale=1.0 / 128), reads=(bankT[7],),
                     writes=(T_ep[4],))

            def t10():
                P.op("act", I("activation", e4, e4, AF.Exp, scale=-0.5), reads=(T_ep[4],), writes=(T_ep[4],))

            def t11():
                P.op("dve", I("scalar_tensor_tensor", bufM[:, h, q0:q0 + nq], e1, gdc, e4, ALU.mult, ALU.mult),
                     reads=(T_ep[1], T_ep[4], T_const), writes=(T_mg[h][qt],))
            return [t0, t1, t2, t3, t4, t5, t6, t7, t8, t9, t10, t11]

        prep(0)
        d_S(d_items[0], 0)
        d_S(d_items[0], 1)
        for j, item in enumerate(d_items):
            ti, i = item
            c = ctx[ti]
            h, qt, nq, st_ = c["hh"], c["qt"], c["nq"], c["st"]
            OBK = [3, 4] if st_ == 0 else [5, 6]
            if i == 0 and ti + 1 < len(tasks):
                prep(ti + 1)
            kb = c["kbs"][i]
            _, nk, kp, c0, diag, mw, bi = geom(qt, kb)
            nxt = d_items[j + 1] if j + 1 < len(d_items) else None
            pis = []
            for m in range(2):
                sbk = sb_of[(item, m)]
                pi = ctr["P"] % 4
                ctr["P"] += 1
                pis.append(pi)
                P.op("act", I("activation", Pt[pi][0:nk, c0:nq], banks[sbk][0:nk, c0:nq], AF.Exp,
                              bias=biasT[0:nk, h, bi:bi + 1], scale=0.125),
                     reads=(bankT[sbk], T_const), writes=(T_P[pi],))
                if nxt is not None:
                    d_S(nxt, m)
            for m in range(2):
                pi = pis[m]
                eng_ = "dve" if m == 0 else ACC2_ENG
                if i == 0:
                    assert nk == 128 and c0 == 0
                    P.op(eng_, I("tensor_copy", Pacc[st_][m][:, 0:nq], Pt[pi][:, 0:nq]), reads=(T_P[pi],),
                         writes=(T_Pacc[st_][m],))
                else:
                    P.op(eng_, I("tensor_tensor", Pacc[st_][m][0:nk, c0:nq], Pacc[st_][m][0:nk, c0:nq],
                                 Pt[pi][0:nk, c0:nq], ALU.add), reads=(T_P[pi], T_Pacc[st_][m]),
                         writes=(T_Pacc[st_][m],))
            for m in range(2):
                pi = pis[m]
                P.op("pe", I("matmul", banks[OBK[m]][:, c0:nq], lhsT=vtk[0:nk, kb, 128 * h:128 * h + 128],
                             rhs=Pt[pi][0:nk, c0:nq], start=(i == 0), stop=(i == c["nkb"] - 1)),
                     reads=(T_P[pi], T_v[kb]), writes=(bankT[OBK[m]],))
            run_pending(1)
            if i == c["nkb"] - 1:
                run_pending(len(pending))
                pending.extend(d_epi(c))
        run_pending(len(pending))

        zb_of = {}
        hist = {}

        def s_zterm(bank, item, start, stop):
            ti, i = item
            c = ctx[ti]
            g, qt, nq, st_ = c["hh"], c["qt"], c["nq"], c["st"]
            kb = c["kbs"][i]
            _, nk, kp, c0, diag, mw, bi = geom(qt, kb)
            ck = 12 + g // 2
            par = g % 2
            ins = [I("matmul", bank[0:nk, c0:nq], lhsT=qkT[:, ck, kp:kp + nk],
                     rhs=QP[st_][par][:, c0:nq], start=start, stop=(stop and not diag))]
            if diag:
                ins.append(I("matmul", bank[0:nk, c0:c0 + mw], lhsT=identb[0:nk, 0:nk],
                             rhs=maskSp[0:nk, 0:mw], start=False, stop=stop))
            return ins, (T_qk[ck][tile_of_block(kb)], T_QP[st_][par], T_const)

        def s_Z(item):
            zb = ctr["Z"] % 2
            ctr["Z"] += 1
            ins, rd = s_zterm(banks[zb], item, True, True)
            P.op("pe", ins, reads=rd, writes=(bankT[zb],))
            zb_of[item] = zb

        def s_epi(c):
            g, qt, q0, nq, st_ = c["hh"], c["qt"], c["q0"], c["nq"], c["st"]
            OB = 5 + st_
            r0 = 64 * (g % 2)
            rr = slice(r0, r0 + 64)
            e1 = ep[1][rr, 0:nq]
            e4 = ep[4][rr, 0:nq]

            def t0():
                P.op("dve", I("tensor_copy", e1, banks[OB][rr, 0:nq]), reads=(bankT[OB],), writes=(T_ep[1],))

            def t1():
                P.op("pool", I("tensor_tensor", sqb[rr, 0:nq], e1, e1, ALU.mult), reads=(T_ep[1],), writes=(T_sq,))

            def t2():
                P.op("pe", I("matmul", banks[7][:, 0:nq], lhsT=onesb[rr, :], rhs=sqb[rr, 0:nq], start=True,
                             stop=True), reads=(T_sq, T_const), writes=(bankT[7],))

            def t3():
                P.op("act", I("activation", e4, banks[7][rr, 0:nq], AF.Ln, bias=EPS, scale=1.0 / 64),
                     reads=(bankT[7],), writes=(T_ep[4],))

            def t4():
                P.op("act", I("activation", e4, e4, AF.Exp, scale=-0.5), reads=(T_ep[4],), writes=(T_ep[4],))

            def t5():
                P.op("dve", I("scalar_tensor_tensor", bufM[rr, 4 + g // 2, q0:q0 + nq], e1, gsbc[rr], e4,
                              ALU.mult, ALU.mult), reads=(T_ep[1], T_ep[4], T_const),
                     writes=(T_mg[4 + g // 2][qt],))
            return [t0, t1, t2, t3, t4, t5]

        def s_stage2(item):
            ti, i = item
            c = ctx[ti]
            g, nq, st_ = c["hh"], c["nq"], c["st"]
            OB = 5 + st_
            vc = 512 + 128 * (g // 2)
            (kb2, nk2, c02, csb2) = hist[item]
            ai = ctr["A"] % 3
            ctr["A"] += 1
            bkw2 = dict(bias=metab) if kb2 == 0 else {}
            P.op("act", I("activation", At[ai][0:nk2, c02:nq], banks[csb2][0:nk2, c02:nq], AF.Exp,
                          scale=-1.0, **bkw2), reads=(bankT[csb2], T_const), writes=(T_A[ai],))
            P.op("pe", I("matmul", banks[OB][:, c02:nq], lhsT=vtk[0:nk2, kb2, vc:vc + 128],
                         rhs=At[ai][0:nk2, c02:nq], start=False, stop=(i == c["nkb"] - 1)),
                 reads=(T_A[ai], T_v[kb2]), writes=(bankT[OB],))
            if i == c["nkb"] - 1:
                run_pending(len(pending))
                pending.extend(s_epi(c))

        s_Z(s_items[0])
        for j in range(len(s_items) + 2):
            if j < len(s_items) and s_items[j][1] == 0 and s_items[j][0] + 1 < len(tasks):
                prep(s_items[j][0] + 1)
            if j + 1 < len(s_items):
                s_Z(s_items[j + 1])
            if j < len(s_items):
                item = s_items[j]
                ti, i = item
                c = ctx[ti]
                g, qt, nq, st_ = c["hh"], c["qt"], c["nq"], c["st"]
                kb = c["kbs"][i]
                _, nk, kp, c0, diag, mw, bi = geom(qt, kb)
                if i == 0:
                    P.op("dve", I("memset", RS32, 0.0), writes=(T_RS32,))
                zb = zb_of[item]
                ei = ctr["E"] % 2
                ctr["E"] += 1
                bkw = dict(bias=metab) if kb == 0 else {}
                P.op("act", I("activation", Et[ei][0:nk, c0:nq], banks[zb][0:nk, c0:nq], AF.Exp, scale=-1.0,
                              **bkw), reads=(bankT[zb], T_const), writes=(T_E[ei],))
            if j >= 2:
                it2 = s_items[j - 2]
                if it2[1] == 0:
                    c2 = ctx[it2[0]]
                    P.op("pe", I("matmul", banks[5 + c2["st"]][:, 0:c2["nq"]], lhsT=zerosb,
                                 rhs=qkT[:, 0, 0:c2["nq"]], start=True, stop=False),
                         reads=(T_const, T_qk[0][0]), writes=(bankT[5 + c2["st"]],))
                s_stage2(it2)
            if j < len(s_items):
                si = ctr["SP"] % 3
                ctr["SP"] += 1
                P.op("act", I("activation", SPt[si][0:nk, c0:nq], Et[ei][0:nk, c0:nq], AF.Ln, bias=1.0,
                              scale=1.0), reads=(T_E[ei],), writes=(T_SP[si],))
                csb = [2, 3, 4][ctr["CS"] % 3]
                ctr["CS"] += 1
                ins, rd = s_zterm(banks[csb], item, True, False)
                P.op("pe", ins, reads=rd, writes=(bankT[csb],))
                rb = (ctr["R"] - 1) % 2
                ins = [I("matmul", banks[csb][0:nk, c0:nq], lhsT=trib[0:nk, 0:nk], rhs=SPt[si][0:nk, c0:nq],
                         start=False, stop=(i == 0))]
                rd = [T_SP[si], T_const]
                if i > 0:
                    ins.append(I("matmul", banks[csb][0:nk, c0:nq], lhsT=onesb[:, 0:nk],
                                 rhs=RSb[rb][:, c0:nq], start=False, stop=True))
                    rd.append(T_RSb[rb])
                P.op("pe", ins, reads=rd, writes=(bankT[csb],))
                if i < c["nkb"] - 1:
                    c0n = geom(qt, c["kbs"][i + 1])[3]
                    P.op("dve", I("tensor_tensor", RS32[0:nk, c0:nq], RS32[0:nk, c0:nq],
                                  SPt[si][0:nk, c0:nq], ALU.add), reads=(T_SP[si], T_RS32), writes=(T_RS32,))
                    rbn = ctr["R"] % 2
                    ctr["R"] += 1
                    P.op("dve", I("tensor_copy", RSb[rbn][:, c0n:nq], RS32[:, c0n:nq]),
                         reads=(T_RS32,), writes=(T_RSb[rbn],))
                hist[item] = (kb, nk, c0, csb)
            run_pending(1)
        run_pending(len(pending))
        P.barrier()
        if dbg and s == 0:
            dsd = DSem(P)
            P.op("sp", I("dma_start", out=dbg_d["mg"], in_=flat(O_M, 8 * L // 2, BF16)), ds=dsd)
            P.barrier()

        oE = O_E
        xin = [flat(oE + 1024 * i, 1024) for i in range(3)]
        T_xin = [Trk() for _ in range(3)]
        xs2 = [flat(oE + 3072 + 512 * i, 512, BF16) for i in range(2)]
        T_xs2 = [Trk() for _ in range(2)]
        junk = flat(oE + 4096, 512, BF16)
        T_junk = Trk()
        g_bc = flat(oE + 4608, 1024)
        T_g = Trk()
        ssb = flat(oE + 5632, 64)
        T_ss = [Trk() for _ in range(4)]
        T_rs = [Trk() for _ in range(4)]
        load_bc(g_bc, g_ffn_d, T_g)
        T_wo = [Trk() for _ in range(4)]
        for sl in range(4):
            P.op("pool", I("dma_start", out=wout[:, :, 256 * sl:256 * sl + 256],
                           in_=w_out_v[:, :, 256 * sl:256 * sl + 256]), writes=(T_wo[sl],), ds=sem_w[sl])
        T_h1 = [Trk() for _ in range(NB)]
        T_hn = [Trk() for _ in range(NB)]
        pb = 0
        ob_of = {}

        def c1_mm(tb):
            nonlocal pb
            p0, n = blk(tb)
            load_x(s, tb, xin[tb % 3], T_xin[tb % 3], sem_x[tb % 3])
            qtile_i = tile_of_block(tb)
            lst = []
            for half in range(2):
                b = pb % 4
                pb += 1
                P.op("pe", [I("matmul", banks[b][0:n, :], lhsT=bufM[:, cc, p0:p0 + n],
                              rhs=wout[:, cc, 512 * half:512 * half + 512], start=(cc == 0), stop=(cc == 7))
                            for cc in range(8)],
                     reads=[T_mg[c][qtile_i] for c in range(8)] + [T_wo[2 * half], T_wo[2 * half + 1]],
                     writes=(bankT[b],))
                lst.append(b)
            ob_of[tb] = lst

        c1_mm(0)
        prev_fin = None
        for tb in range(NB):
            p0, n = blk(tb)
            i3, i2, k4 = tb % 3, tb % 2, tb % 4
            for half in range(2):
                b = ob_of[tb][half]
                P.op("dve", I("tensor_tensor", h1[0:n, tb, 512 * half:512 * half + 512], banks[b][0:n, :],
                              xin[i3][0:n, 512 * half:512 * half + 512], ALU.add),
                     reads=(bankT[b], T_xin[i3]), writes=(T_h1[tb],))
            if tb + 1 < NB:
                c1_mm(tb + 1)
            fin = norm_transpose(h1[0:n, tb, :], T_h1[tb], n, tb, g_bc, T_g, xs2[i2], T_xs2[i2],
                                 ssb[:, k4:k4 + 1], T_ss[k4], ssb[:, 8 + k4:9 + k4], T_rs[k4], junk, T_junk,
                                 6 + i2, bufX, T_hn[tb])
            if prev_fin is not None:
                prev_fin()
            prev_fin = fin
        prev_fin()
        P.barrier()
        if dbg and s == 0:
            dsd = DSem(P)
            P.op("sp", I("dma_start", out=dbg_d["h1"], in_=flat(O_R, NB * 1024)), ds=dsd)
            P.barrier()

        wup = [[flat(O_M + 4096 * i + 2048 * j, 2048, BF16).rearrange("p (c f) -> p c f", c=8) for j in range(2)]
               for i in range(2)]
        T_wup = [[[Trk(), Trk()] for j in range(2)] for i in range(2)]
        wd = [wd0, flat(oE + 4096, 2048, BF16).rearrange("p (c f) -> p c f", c=4)]
        T_wd = [Trk(), Trk()]
        yt = [[flat(oE + 1536 * i + 512 * j, 512) for j in range(3)] for i in range(2)]
        T_y = [[Trk() for j in range(3)] for i in range(2)]
        g_bc = flat(oE + 3072, 1024)
        T_g = Trk()
        load_bc(g_bc, g_fin_d, T_g)
        ssb = flat(O_RJ, 64)
        T_ss = [Trk() for _ in range(4)]
        T_rs = [Trk() for _ in range(4)]
        junk = flat(O_RJ + 64, 512, BF16)
        T_junk = Trk()
        T_a = [[Trk() for _ in range(NB)] for _ in range(4)]
        groups = [(4 * i, min(4, 22 - 4 * i)) for i in range(6)]
        ftiles = []
        p = 0
        while p < L:
            n = min(510, L - p)
            ftiles.append((p, n, 0 if p == 0 else 2))
            p += n
        pb = 0
        yc = 0
        for gi, (fc0, nch) in enumerate(groups):
            wb = gi % 2
            for j in range(2):
                for sl in range(nch // 2):
                    col = (0 if j == 0 else DFF) + 128 * fc0 + 256 * sl
                    P.op("pool", I("dma_start", out=wup[wb][j][:, :, 256 * sl:256 * sl + 256],
                                   in_=w_up_v[:, :, col:col + 256]), writes=(T_wup[wb][j][sl],), ds=sem_w[2 * j + sl])
            P.op("pool", I("dma_start", out=wd[wb][:, 0:nch, :], in_=w_dn_v[:, fc0:fc0 + nch, :]),
                 writes=(T_wd[wb],), ds=sem_w[wb])
            for jc in range(nch):
                c = fc0 + jc
                for (p0, n, ho) in ftiles:
                    hb = []
                    tbs = blocks_of(p0 - ho, n + ho)
                    for j in range(2):
                        b = pb % 4
                        pb += 1
                        hb.append(b)
                        P.op("pe", [I("matmul", banks[b][:, 0:n + ho], lhsT=wup[wb][j][:, cc, 128 * jc:128 * jc + 128],
                                      rhs=bufX[:, cc, p0 - ho:p0 + n], start=(cc == 0), stop=(cc == 7))
                                    for cc in range(8)],
                             reads=[T_wup[wb][j][jc // 2]] + [T_hn[tb] for tb in tbs], writes=(bankT[b],))
                    ys = yc % 2
                    yc += 1
                    for j in range(2):
                        b = hb[j]
                        cc_ = c + 22 * j
                        y = yt[ys][j]
                        P.op("act", I("activation", y[:, 0:n], banks[b][:, ho:ho + n], AF.Identity,
                                      bias=cw[:, 3, cc_:cc_ + 1], scale=cw[:, 2, cc_:cc_ + 1]),
                             reads=(bankT[b], T_const), writes=(T_y[ys][j],))
                        j1 = 0 if ho == 2 else 1
                        P.op("dve", I("scalar_tensor_tensor", y[:, j1:n], banks[b][:, ho + j1 - 1:ho + n - 1],
                                      cw[:, 1, cc_:cc_ + 1], y[:, j1:n], ALU.mult, ALU.add),
                             reads=(bankT[b], T_y[ys][j], T_const), writes=(T_y[ys][j],))
                        j2 = 0 if ho == 2 else 2
                        P.op("dve", I("scalar_tensor_tensor", y[:, j2:n], banks[b][:, ho + j2 - 2:ho + n - 2],
                                      cw[:, 0, cc_:cc_ + 1], y[:, j2:n], ALU.mult, ALU.add),
                             reads=(bankT[b], T_y[ys][j], T_const), writes=(T_y[ys][j],))
                    P.op("act", I("activation", yt[ys][2][:, 0:n], yt[ys][0][:, 0:n], AF.Silu),
                         reads=(T_y[ys][0],), writes=(T_y[ys][2],))
                    tbs = blocks_of(p0, n)
                    P.op("pool", I("tensor_tensor", a_g[:, jc, p0:p0 + n], yt[ys][2][:, 0:n], yt[ys][1][:, 0:n],
                                   ALU.mult), reads=(T_y[ys][2], T_y[ys][1]), writes=[T_a[jc][tb] for tb in tbs])
            last = gi == len(groups) - 1
            for tb in range(1, NB):
                p0, n = blk(tb)
                k4 = tb % 4
                for half in range(2):
                    b = 4 + pb % 2
                    pb += 1
                    P.op("pe", [I("matmul", banks[b][0:n, :], lhsT=a_g[:, jc, p0:p0 + n],
                                  rhs=wd[wb][:, jc, 512 * half:512 * half + 512], start=(jc == 0),
                                  stop=(jc == nch - 1)) for jc in range(nch)],
                         reads=[T_a[jc][tb] for jc in range(nch)] + [T_wd[wb]], writes=(bankT[b],))
                    P.op("dve", I("tensor_tensor", h1[0:n, tb, 512 * half:512 * half + 512], banks[b][0:n, :],
                                  h1[0:n, tb, 512 * half:512 * half + 512], ALU.add),
                         reads=(bankT[b], T_h1[tb]), writes=(T_h1[tb],))
                if last:
                    ss = ssb[:, k4:k4 + 1]
                    rs = ssb[:, 8 + k4:9 + k4]
                    P.op("act", I("activation", junk[0:n], h1[0:n, tb, :], AF.Square, accum_out=ss[0:n]),
                         reads=(T_h1[tb],), writes=(T_junk, T_ss[k4]))
                    rstd_ops(ss, rs, n, T_ss[k4], T_rs[k4], 1.0 / D)
                    P.op("dve", I("scalar_tensor_tensor", h1[0:n, tb, :], h1[0:n, tb, :], rs[0:n], g_bc[0:n],
                                  ALU.mult, ALU.mult), reads=(T_h1[tb], T_rs[k4], T_g), writes=(T_h1[tb],))
                    P.op("sp", I("dma_start", out=out_d[s, 128 * (tb - 1):128 * tb, :], in_=h1[0:128, tb, :]),
                         reads=(T_h1[tb],), ds=sem_o)
        P.barrier()

    P.emit(nc)
    st.close()
    return nc


_CACHE = {}


def prep_inputs(inputs):
    f = lambda a: np.ascontiguousarray(np.asarray(a, dtype=np.float32))
    shared = {
        "meta": f(inputs["meta_tokens"]),
        "w_in": f(inputs["w_in"][0]),
        "w_out": f(inputs["w_out"][0]),
        "w_up": f(inputs["w_up"][0]),
        "w_down": f(inputs["w_down"][0]),
        "convw": f(np.concatenate([np.asarray(inputs["conv_w"][0]), np.asarray(inputs["conv_b"])], axis=0)),
        "g_attn": f(inputs["g_attn"]),
        "g_ffn": f(inputs["g_ffn"]),
        "g_fin": f(np.asarray(inputs["g_final"]).reshape(1, D)),
        "g_diff": f(inputs["g_diff"]),
        "g_sb": f(inputs["g_sb"]),
        "lam": f(np.concatenate([np.asarray(inputs[k]) for k in ("lam_q1", "lam_k1", "lam_q2", "lam_k2")], axis=0)),
    }
    return shared


def kernel(**inputs):
    x = np.asarray(inputs["x"], dtype=np.float32)
    B, SEQ, _ = x.shape
    ncores = 8
    nseq = B // ncores
    key = (nseq, SEQ)
    if key not in _CACHE:
        _CACHE[key] = build(nseq, SEQ)
    nc = _CACHE[key]
    shared = prep_inputs(inputs)
    in_maps = []
    for c in range(ncores):
        m = dict(shared)
        m["x"] = np.ascontiguousarray(x[c * nseq:(c + 1) * nseq])
        in_maps.append(m)
    res = run_bass_kernel_spmd(nc, in_maps, core_ids=list(range(ncores)))
    return np.concatenate([np.asarray(r["out"]) for r in res.results], axis=0).astype(np.float32)
```

```python
import numpy as np
import concourse.bass as bass
import concourse.mybir as mybir
from concourse.bass_utils import run_bass_kernel_spmd

F32 = mybir.dt.float32
BF16 = mybir.dt.bfloat16
I32 = mybir.dt.int32
AF = mybir.ActivationFunctionType
ALU = mybir.AluOpType
AX = mybir.AxisListType

D = 1024
DFF = 2816
NEG = -4000.0
EPS = 1e-6
LAM_INIT = 0.8 - 0.6
SLOPES = [0.25 ** (i + 1) for i in range(4)]
EPOCH = 12000
NDUMMY = 0
ACC2_ENG = "dve"


def I(name, *a, **kw):
    return (name, a, kw)


class Trk:
    __slots__ = ("w", "r", "const", "name")

    def __init__(self, name="", const=False):
        self.w = None
        self.r = {}
        self.const = const
        self.name = name


class DSem:
    def __init__(self, prog):
        self.id = len(prog.dsems)
        self.issued = 0
        prog.dsems.append(self)


class Queue:
    def __init__(self, name):
        self.name = name
        self.ops = []
        self.count = 0
        self.epoch = 0
        self.known = {}
        self.pending = []


class Prog:
    ENG = ("pe", "act", "dve", "pool", "sp")

    def __init__(self):
        self.q = {n: Queue(n) for n in self.ENG}
        self.dsems = []

    def _need(self, q, waits, ev, kind):
        if ev is None:
            return
        key, val, ds = ev
        if ds is not None:
            val = ds.issued * 16
        if key[0] == "E" and key[1] == q.name:
            if q.name == "pe" or kind == "war":
                return
        if q.known.get(key, 0) >= val:
            return
        q.known[key] = val
        waits.append((key, val))

    def op(self, eng, fn, reads=(), writes=(), ds=None):
        if isinstance(fn, tuple):
            fn = [fn]
        q = self.q[eng]
        waits = q.pending
        q.pending = []
        for t in reads:
            self._need(q, waits, t.w, "raw")
        for t in writes:
            self._need(q, waits, t.w, "waw")
            for k, (v, d) in t.r.items():
                self._need(q, waits, (k, v, d), "war")
        if ds is None:
            if q.count >= EPOCH:
                q.epoch += 1
                q.count = 0
            q.count += 1
            ev = (("E", q.name, q.epoch), q.count, None)
            inc = 1
        else:
            ds.issued += 1
            ev = (("D", ds.id), ds.issued * 16, ds)
            inc = 16
        for t in reads:
            if not t.const:
                t.r[ev[0]] = (ev[1], ev[2])
        for t in writes:
            t.w = ev
            t.r = {}
        q.ops.append((waits, fn, ev[0], inc))
        return ev

    def barrier(self):
        evs = []
        for q in self.q.values():
            if q.name == "sp":
                continue
            for ep in range(q.epoch + 1):
                cnt = q.count if ep == q.epoch else EPOCH
                if cnt > 0:
                    evs.append((("E", q.name, ep), cnt, None))
        for d in self.dsems:
            if d.issued:
                evs.append((("D", d.id), d.issued * 16, d))
        for q in self.q.values():
            for ev in evs:
                if ev[0][0] == "E" and ev[0][1] == q.name:
                    continue
                self._need(q, q.pending, ev, "raw")

    def emit(self, nc):
        keys = []
        for q in self.q.values():
            for (_, _, k, _) in q.ops:
                if k not in keys:
                    keys.append(k)
        sems = {}
        import contextlib
        with contextlib.ExitStack() as st:
            for i, k in enumerate(keys):
                sems[k] = st.enter_context(nc.semaphore("s%d" % i))
            block = st.enter_context(nc.Block())

            def run(e, q):
                for (waits, fn, k, inc) in q.ops:
                    for (wk, wv) in waits:
                        e.wait_ge(sems[wk], wv)
                    ins = None
                    for (nm, a, kw) in fn:
                        ins = getattr(e, nm)(*a, **kw)
                    ins.then_inc(sems[k], inc)
                for (wk, wv) in q.pending:
                    e.wait_ge(sems[wk], wv)

            @block.tensor
            def _(e):
                run(e, self.q["pe"])

            @block.scalar
            def _(e):
                run(e, self.q["act"])

            @block.vector
            def _(e):
                run(e, self.q["dve"])

            @block.gpsimd
            def _(e):
                run(e, self.q["pool"])

            @block.sync
            def _(e):
                run(e, self.q["sp"])


def build(nseq, SEQ, dbg=False):
    assert SEQ % 512 == 0
    L = SEQ + 16
    NT = SEQ // 512
    NBF = SEQ // 128
    NB = NBF + 1
    nc = bass.Bass("TRN2", target_bir_lowering=False)
    P = Prog()

    def din(name, shape):
        return nc.dram_tensor(name, shape, F32, kind="ExternalInput").ap()

    x_d = din("x", [nseq, SEQ, D])
    meta_d = din("meta", [16, D])
    w_in_d = din("w_in", [D, 3072])
    w_out_d = din("w_out", [D, D])
    w_up_d = din("w_up", [D, 2 * DFF])
    w_down_d = din("w_down", [DFF, D])
    convw_d = din("convw", [4, 2 * DFF])
    g_attn_d = din("g_attn", [1, D])
    g_ffn_d = din("g_ffn", [1, D])
    g_fin_d = din("g_fin", [1, D])
    g_diff_d = din("g_diff", [1, 128])
    g_sb_d = din("g_sb", [1, 64])
    lam_d = din("lam", [4, 64])
    out_d = nc.dram_tensor("out", [nseq, SEQ, D], F32, kind="ExternalOutput").ap()
    dbg_d = {}
    if dbg:
        dbg_d["qkT"] = nc.dram_tensor("dbg_qkT", [128, 16 * L], BF16, kind="ExternalOutput").ap()
        dbg_d["v"] = nc.dram_tensor("dbg_v", [128, NB * 1024], BF16, kind="ExternalOutput").ap()
        dbg_d["mg"] = nc.dram_tensor("dbg_mg", [128, 8 * L], BF16, kind="ExternalOutput").ap()
        dbg_d["h1"] = nc.dram_tensor("dbg_h1", [128, NB * 1024], F32, kind="ExternalOutput").ap()

    import contextlib
    st = contextlib.ExitStack()

    def sb(name, shape, dt):
        return st.enter_context(nc.sbuf_tensor(name, shape, dt))

    W_X = max(8 * L // 2, 8256)
    W_R = max(16 * L // 2 + NB * 512, NB * 1024 + 4 * L // 2 + 2048 + 640, NB * 1024 + 4096)
    W_E = 6144
    arena = sb("arena", [128, W_R + 2 * W_X + W_E], F32)[:]
    O_R, O_X, O_M, O_E = 0, W_R, W_R + W_X, W_R + 2 * W_X

    def view(off, shape, dt):
        n = int(np.prod(shape))
        nw = n if dt == F32 else n // 2
        ap = arena[:, off:off + nw]
        if dt != F32:
            ap = ap.bitcast(dt)
        return ap.rearrange("p (a b) -> p a b", a=shape[0])

    def flat(off, nwords, dt=F32):
        ap = arena[:, off:off + nwords]
        return ap if dt == F32 else ap.bitcast(dt)

    qkT = view(O_R, [16, L], BF16)
    vtk = view(O_R + 16 * L // 2, [NB, 1024], BF16)
    h1 = view(O_R, [NB, 1024], F32)
    a_g = view(O_R + NB * 1024, [4, L], BF16)
    wd0 = view(O_R + NB * 1024 + 4 * L // 2, [4, 1024], BF16)
    O_RJ = O_R + NB * 1024 + 4 * L // 2 + 2048
    wout = view(O_R + NB * 1024, [8, 1024], BF16)
    bufX = view(O_X, [8, L], BF16)
    bufM = view(O_M, [8, L], BF16)

    identb = sb("identb", [128, 128], BF16)[:]
    identf = sb("identf", [128, 128], F32)[:]
    trib = sb("trib", [128, 128], BF16)[:]
    onesb = sb("onesb", [128, 128], BF16)[:]
    zerosb = sb("zerosb", [128, 128], BF16)[:]
    maskD = sb("maskD", [128, 128], BF16)[:]
    maskS = sb("maskS", [128, 128], BF16)[:]
    maskSp = sb("maskSp", [128, 128], BF16)[:]
    tmpf = sb("tmpf", [128, 128], F32)[:]
    biasT = sb("biasT", [128, 4, 25], F32)[:]
    metab = sb("metab", [128, 1], F32)[:]
    iot = sb("iot", [128, 25], I32)[:]
    iotf = sb("iotf", [128, 25], F32)[:]
    cw = sb("cw", [128, 4, 44], F32)[:]
    cwrow = sb("cwrow", [44, 4, 128], F32)[:]
    cols = sb("cols", [128, 8], F32)[:]
    rows = sb("rows", [1, 640], F32)[:]
    onesrow = sb("onesrow", [1, 128], F32)[:]
    onesf = sb("onesf", [128, 128], F32)[:]
    onesrowb = sb("onesrowb", [1, 128], BF16)[:]
    qb8 = sb("qb8", [1, 512], BF16)[:]
    onesr0 = sb("onesr0", [128, 128], BF16)[:]
    qb8p = sb("qb8p", [128, 512], BF16)[:]
    qb8i = sb("qb8i", [1, 512], I32)[:]
    qb8f = sb("qb8f", [1, 512], F32)[:]

    banks = [st.enter_context(nc.psum_tensor("bank%d" % i, [128, 512], F32))[:] for i in range(8)]
    bankT = [Trk("bank%d" % i) for i in range(8)]

    T_const = Trk("const")
    sem_setup = DSem(P)

    def pl(ins):
        P.op("pool", ins, writes=(T_const,))

    pl(I("memset", tmpf, 1.0))
    pl(I("affine_select", identf, tmpf, [[-1, 128]], ALU.is_equal, 0.0, base=0, channel_multiplier=1))
    pl(I("tensor_copy", identb, identf))
    pl(I("affine_select", identf, tmpf, [[-1, 128]], ALU.is_ge, 0.0, base=0, channel_multiplier=1))
    pl(I("tensor_copy", trib, identf))
    pl(I("memset", onesb, 1.0))
    pl(I("memset", zerosb, 0.0))
    pl(I("memset", onesrow, 1.0))
    pl(I("memset", onesf, 1.0))
    pl(I("memset", onesrowb, 1.0))
    pl(I("iota", qb8i, [[-2, 512]], base=512, channel_multiplier=0))
    pl(I("tensor_copy", qb8f, qb8i))
    pl(I("tensor_copy", qb8, qb8f))
    pl(I("memset", onesr0, 0.0))
    pl(I("memset", onesr0[0:1, :], 1.0))
    pl(I("memset", qb8p, 0.0))
    pl(I("tensor_copy", qb8p[0:1, :], qb8f))
    pl(I("memset", tmpf, 0.0))
    pl(I("affine_select", identf, tmpf, [[1, 128]], ALU.is_ge, NEG, base=0, channel_multiplier=-1))
    pl(I("tensor_copy", maskD, identf))
    pl(I("affine_select", identf, tmpf, [[1, 128]], ALU.is_gt, NEG, base=0, channel_multiplier=-1))
    pl(I("tensor_copy", maskS, identf))
    pl(I("affine_select", identf, tmpf, [[1, 128]], ALU.is_gt, -NEG, base=0, channel_multiplier=-1))
    pl(I("tensor_copy", maskSp, identf))
    pl(I("memset", tmpf, 1.0))
    pl(I("affine_select", identf, tmpf, [[-1, 128]], ALU.is_equal, 0.0, base=0, channel_multiplier=1))
    pl(I("iota", iot[:, 0:20], [[128, 20]], base=-16 * 128 - 256, channel_multiplier=1))
    pl(I("iota", iot[:, 20:24], [[-512, 4]], base=-272, channel_multiplier=1))
    pl(I("tensor_copy", iotf[:, 0:24], iot[:, 0:24]))
    pl(I("tensor_copy", iotf[:, 24:25], iotf[:, 16:17]))
    for h in range(4):
        pl(I("tensor_scalar", biasT[:, h, :], iotf, float(SLOPES[h]), None, ALU.mult))
    pl(I("tensor_scalar", metab, iotf[:, 16:17], -240.5, -30000.0, ALU.is_gt, ALU.mult))
    for h in range(4):
        pl(I("tensor_scalar", biasT[:, h, 20:25], biasT[:, h, 20:25], metab[:, 0:1], None, ALU.add))
    T_ld = Trk("ld")

    def ld(out, in_):
        P.op("sp", I("dma_start", out=out, in_=in_), writes=(T_ld,), ds=sem_setup)

    for k in range(4):
        ld(cwrow[:, k, :], convw_d[k].rearrange("(c p) -> c p", p=128))
    ld(rows[:, 0:128], g_diff_d)
    ld(rows[:, 128:192], g_sb_d)
    ld(rows[:, 192:256], g_sb_d)
    ld(rows[:, 256:512], lam_d.rearrange("(o a) b -> o (a b)", o=1))
    for k in range(4):
        P.op("pe", I("matmul", banks[0][:, 64 * k:64 * k + 44], lhsT=cwrow[:, k, :], rhs=identf[0:44, 0:44],
                     start=True, stop=True), reads=(T_ld, T_const), writes=(bankT[0],))
    for k in range(4):
        P.op("dve", I("tensor_copy", cw[:, k, :], banks[0][:, 64 * k:64 * k + 44]),
             reads=(bankT[0],), writes=(T_const,))
    P.op("pe", I("matmul", banks[1][:, 0:1], lhsT=rows[0:1, 0:128], rhs=identf[0:1, 0:1], start=True, stop=True),
         reads=(T_ld, T_const), writes=(bankT[1],))
    P.op("pe", I("matmul", banks[1][:, 1:2], lhsT=rows[0:1, 128:256], rhs=identf[0:1, 0:1], start=True, stop=True),
         reads=(T_ld, T_const), writes=(bankT[1],))
    T_l = Trk("lam")
    P.op("dve", I("tensor_tensor", rows[:, 512:576], rows[:, 256:320], rows[:, 320:384], ALU.mult),
         reads=(T_ld,), writes=(T_l,))
    P.op("dve", I("tensor_tensor", rows[:, 576:640], rows[:, 384:448], rows[:, 448:512], ALU.mult),
         reads=(T_l, T_ld), writes=(T_l,))
    P.op("dve", I("reduce_sum", rows[:, 256:258], rows[:, 512:640].rearrange("o (a b) -> o a b", a=2), AX.X),
         reads=(T_l,), writes=(T_l,))
    P.op("act", I("activation", rows[:, 260:262], rows[:, 256:258], AF.Exp), reads=(T_l,), writes=(T_l,))
    P.op("dve", I("tensor_tensor", rows[:, 264:265], rows[:, 261:262], rows[:, 260:261], ALU.subtract),
         reads=(T_l,), writes=(T_l,))
    P.op("dve", I("tensor_scalar", rows[:, 266:267], rows[:, 264:265], -LAM_INIT, None, ALU.add),
         reads=(T_l,), writes=(T_l,))
    P.op("pe", I("matmul", banks[1][:, 2:3], lhsT=onesrow[0:1, :], rhs=rows[0:1, 266:267], start=True, stop=True),
         reads=(T_l, T_const), writes=(bankT[1],))
    P.op("dve", I("tensor_copy", cols[:, 0:3], banks[1][:, 0:3]), reads=(bankT[1],), writes=(T_const,))
    P.op("dve", I("tensor_scalar", cols[:, 0:1], cols[:, 0:1], 1.0 - LAM_INIT, None, ALU.mult),
         reads=(T_const,), writes=(T_const,))
    P.barrier()
    T_const.const = True
    gdc = cols[:, 0:1]
    gsbc = cols[:, 1:2]
    nlamc = cols[:, 2:3]

    sem_g = DSem(P)

    def load_bc(dst, src, trk):
        P.op("sp", I("dma_start", out=dst, in_=bass.AP(src.tensor, 0, [[0, 128], [1, D]])), writes=(trk,), ds=sem_g)

    def blk(tb):
        return (0, 16) if tb == 0 else (16 + 128 * (tb - 1), 128)

    def blocks_of(p0, n):
        return [tb for tb in range(NB) if blk(tb)[0] < p0 + n and blk(tb)[0] + blk(tb)[1] > p0]

    def tile_of_block(tb):
        return 0 if tb == 0 else 1 + (tb - 1) // 4

    def rstd_ops(ss, rs, n, trk_ss, trk_rs, scale):
        P.op("act", I("activation", rs[0:n], ss[0:n], AF.Ln, bias=EPS, scale=scale),
             reads=(trk_ss,), writes=(trk_rs,))
        P.op("act", I("activation", rs[0:n], rs[0:n], AF.Exp, scale=-0.5), reads=(trk_rs,), writes=(trk_rs,))

    sem_x = [DSem(P) for _ in range(3)]
    sem_w = [DSem(P) for _ in range(4)]
    sem_o = DSem(P)
    evac_rr = [0]

    def evac(out, in_, reads, writes):
        evac_rr[0] ^= 1
        if evac_rr[0]:
            P.op("act", I("activation", out, in_, AF.Copy), reads=reads, writes=writes)
        else:
            P.op("dve", I("tensor_copy", out, in_), reads=reads, writes=writes)

    def norm_transpose(src_rows, src_trk, n, tb, g_bc, T_g, xs, T_xs, ss, T_ss, rs, T_rs, junk, T_junk,
                       pbank, dst, T_dst):
        p0 = blk(tb)[0]
        P.op("act", I("activation", junk[0:n], src_rows, AF.Square, accum_out=ss[0:n]),
             reads=(src_trk,), writes=(T_junk, T_ss))
        rstd_ops(ss, rs, n, T_ss, T_rs, 1.0 / D)
        P.op("dve", I("scalar_tensor_tensor", xs[0:n], src_rows, rs[0:n], g_bc[0:n], ALU.mult, ALU.mult),
             reads=(src_trk, T_rs, T_g), writes=(T_xs,))
        pT = banks[pbank].bitcast(BF16).rearrange("p (c t) -> p c t", c=8)
        P.op("pe", [I("transpose", pT[:, c, 0:n], xs[0:n, 128 * c:128 * (c + 1)], identb[0:n, 0:n])
                    for c in range(8)], reads=(T_xs, T_const), writes=(bankT[pbank],))
        return lambda: evac(dst[:, :, p0:p0 + n], pT[:, :, 0:n], (bankT[pbank],), (T_dst,))

    def load_x(s, tb, dst, trk, ds):
        if tb == 0:
            P.op("sp", I("dma_start", out=dst[0:16], in_=meta_d), writes=(trk,), ds=ds)
        else:
            P.op("sp", I("dma_start", out=dst[0:128], in_=x_d[s, 128 * (tb - 1):128 * tb, :]), writes=(trk,), ds=ds)

    def keyblocks(qt):
        return [0] if qt == 0 else list(range(4 * qt + 1))

    def geom(qt, kb):
        kp, nk = blk(kb)
        nk = 128
        if qt == 0:
            return 16, nk, kp, 0, True, 16, 24
        if kb == 0:
            return 512, nk, kp, 0, False, 128, 19 + qt
        rel = (kb - 1) - 4 * (qt - 1)
        return 512, nk, kp, (128 * rel if rel > 0 else 0), rel >= 0, 128, rel + 16

    def qtile(qt):
        return (0, 16) if qt == 0 else (16 + 512 * (qt - 1), 512)

    w_in_v = w_in_d.rearrange("(cc p) f -> p cc f", p=128)
    w_out_v = w_out_d.rearrange("(cc p) f -> p cc f", p=128)
    w_up_v = w_up_d.rearrange("(cc p) f -> p cc f", p=128)
    w_dn_v = w_down_d.rearrange("(fc p) d -> p fc d", p=128)
    ttiles = [qtile(qt) for qt in range(NT + 1)]

    for s in range(nseq):
        oM = O_M
        xin = [flat(oM + 1024 * i, 1024) for i in range(3)]
        T_xin = [Trk() for _ in range(3)]
        xs2 = [flat(oM + 3072 + 512 * i, 512, BF16) for i in range(2)]
        T_xs2 = [Trk() for _ in range(2)]
        junk = flat(oM + 4096, 512, BF16)
        T_junk = Trk()
        wsl = [flat(oM + 4608 + 1024 * i, 1024, BF16).rearrange("p (c f) -> p c f", c=8) for i in range(3)]
        T_wsl = [Trk() for _ in range(3)]
        ssb = flat(O_E + 4096, 64)
        T_ss = [Trk() for _ in range(4)]
        T_rs = [Trk() for _ in range(4)]
        g_bc = flat(O_E, 1024)
        T_g = Trk()
        load_bc(g_bc, g_attn_d, T_g)
        T_xn = [Trk() for _ in range(NB)]
        T_qk = [[Trk() for _ in range(NT + 1)] for _ in range(16)]
        T_v = [Trk() for _ in range(NB)]
        P.op("pool", I("memset", vtk[:, 0, :], 0.0), writes=(T_v[0],))
        prev_fin = None
        for tb in range(NB):
            p0, n = blk(tb)
            i3, i2, k4 = tb % 3, tb % 2, tb % 4
            load_x(s, tb, xin[i3], T_xin[i3], sem_x[i3])
            fin = norm_transpose(xin[i3][0:n], T_xin[i3], n, tb, g_bc, T_g, xs2[i2], T_xs2[i2],
                                 ssb[:, k4:k4 + 1], T_ss[k4], ssb[:, 8 + k4:9 + k4], T_rs[k4], junk, T_junk,
                                 6 + i2, bufX, T_xn[tb])
            if prev_fin is not None:
                prev_fin()
            prev_fin = fin
        prev_fin()
        pb = 0
        for sl in range(12):
            i3 = sl % 3
            P.op("pool", I("dma_start", out=wsl[i3], in_=w_in_v[:, :, 256 * sl:256 * sl + 256]),
                 writes=(T_wsl[i3],), ds=sem_w[i3])
            c0 = 2 * sl
            if c0 < 8 or 12 <= c0 < 20:
                for fi in range(2):
                    c = c0 + fi
                    qc = c if c < 8 else c - 4
                    for ti, (t0, n) in enumerate(ttiles):
                        b = pb % 6
                        pb += 1
                        blks = blocks_of(t0, n)
                        P.op("pe", [I("matmul", banks[b][:, 0:n], lhsT=wsl[i3][:, cc, 128 * fi:128 * fi + 128],
                                      rhs=bufX[:, cc, t0:t0 + n], start=(cc == 0), stop=(cc == 7))
                                    for cc in range(8)],
                             reads=[T_wsl[i3]] + [T_xn[tb] for tb in blks], writes=(bankT[b],))
                        if 8 <= qc < 12:
                            P.op("dve", I("tensor_scalar", qkT[:, qc, t0:t0 + n], banks[b][:, 0:n], -0.125, None,
                                          ALU.mult), reads=(bankT[b],), writes=(T_qk[qc][ti],))
                        else:
                            evac(qkT[:, qc, t0:t0 + n], banks[b][:, 0:n], (bankT[b],), (T_qk[qc][ti],))
            else:
                vcol = 256 * (sl - 4) if sl < 6 else 512 + 256 * (sl - 10)
                for tb in range(NB):
                    p0, n = blk(tb)
                    b = pb % 6
                    pb += 1
                    P.op("pe", [I("matmul", banks[b][0:n, 0:256], lhsT=bufX[:, cc, p0:p0 + n], rhs=wsl[i3][:, cc, :],
                                  start=(cc == 0), stop=(cc == 7)) for cc in range(8)],
                         reads=(T_wsl[i3], T_xn[tb]), writes=(bankT[b],))
                    evac(vtk[0:n, tb, vcol:vcol + 256], banks[b][0:n, 0:256], (bankT[b],), (T_v[tb],))
        P.barrier()
        if dbg and s == 0:
            dsd = DSem(P)
            P.op("sp", I("dma_start", out=dbg_d["qkT"], in_=flat(O_R, 16 * L // 2, BF16)), ds=dsd)
            P.op("sp", I("dma_start", out=dbg_d["v"], in_=flat(O_R + 16 * L // 2, NB * 512, BF16)), ds=dsd)
            P.barrier()

        T_mg = [[Trk() for _ in range(NT + 1)] for _ in range(8)]
        o = O_X
        Pt = [flat(o + 256 * i, 256, BF16) for i in range(4)]
        T_P = [Trk() for _ in range(4)]
        o += 1024
        Et = [flat(o + 512 * i, 512) for i in range(2)]
        T_E = [Trk() for _ in range(2)]
        o += 1024
        SPt = [flat(o + 256 * i, 256, BF16) for i in range(3)]
        T_SP = [Trk() for _ in range(3)]
        o += 768
        QP = [[flat(o + 512 * i + 256 * j, 256, BF16) for j in range(2)] for i in range(2)]
        T_QP = [[Trk() for j in range(2)] for i in range(2)]
        o += 1024
        At = [flat(o + 256 * i, 256, BF16) for i in range(3)]
        T_A = [Trk() for _ in range(3)]
        o += 768
        RS32 = flat(o, 512)
        T_RS32 = Trk()
        o += 512
        RSb = [flat(o + 256 * i, 256, BF16) for i in range(2)]
        T_RSb = [Trk() for _ in range(2)]
        o += 512
        Pacc = [[flat(o + 1024 * i + 512 * j, 512) for j in range(2)] for i in range(2)]
        T_Pacc = [[Trk() for j in range(2)] for i in range(2)]
        o += 2048
        assert o - O_X <= W_X, (o - O_X, W_X)
        ep = [flat(O_E + 512 * i, 512) for i in range(5)]
        T_ep = [Trk() for _ in range(5)]
        sqb = flat(O_E + 2560, 256, BF16)
        T_sq = Trk()
        for i_ in range(2):
            for j_ in range(2):
                P.op("pool", I("memset", QP[i_][j_], 0.0), writes=(T_QP[i_][j_],))

        om = O_E + 2816
        QPm = flat(om, 128, BF16).rearrange("p (s q) -> p s q", s=16)
        Pm = flat(om + 128, 64, BF16)
        Em = flat(om + 192, 128)
        SPm = flat(om + 320, 64, BF16)
        Am = flat(om + 384, 64, BF16)
        T_m = Trk()
        P.op("dve", I("memset", QPm, 0.0), writes=(T_m,))
        for m in range(2):
            P.op("dve", I("tensor_copy", QPm[64 * m:64 * m + 64, m:8:2, :], qkT[64 * m:64 * m + 64, 0:4, 0:16]),
                 reads=[T_qk[c][0] for c in range(4)], writes=(T_m,))
            P.op("dve", I("tensor_copy", QPm[64 * m:64 * m + 64, 8 + 4 * m:12 + 4 * m, :],
                          qkT[64 * m:64 * m + 64, 8:12, 0:16]), reads=[T_qk[c][0] for c in range(8, 12)],
                 writes=(T_m,))
        _rec = []
        _orig_op = P.op
        P.op = lambda *a_, **k_: _rec.append((a_, k_))
        T_epm = [Trk() for _ in range(5)]
        T_sqm = Trk()
        for h in range(4):
            for m in range(2):
                cs_ = 16 * (2 * h + m)
                ins = [I("matmul", banks[0][:, cs_:cs_ + 16], lhsT=qkT[:, 4 + h, 0:128], rhs=QPm[:, 2 * h + m, :],
                         start=True, stop=False)]
                if h == 0:
                    ins.append(I("matmul", banks[0][:, cs_:cs_ + 16], lhsT=onesr0, rhs=qb8p[:, 0:16], start=False,
                                 stop=False))
                ins.append(I("matmul", banks[0][:, cs_:cs_ + 16], lhsT=identb, rhs=maskD[:, 0:16], start=False,
                             stop=True))
                P.op("pe", ins, reads=(T_qk[4 + h][0], T_m, T_const), writes=(bankT[0],))
        T_Pm = Trk()
        for h in range(4):
            P.op("act", I("activation", Pm[:, 32 * h:32 * h + 32], banks[0][:, 32 * h:32 * h + 32], AF.Exp,
                          bias=biasT[:, h, 24:25], scale=0.125), reads=(bankT[0], T_const), writes=(T_Pm,))
        P.op("pe", I("matmul", banks[7][:, 0:128], lhsT=onesb, rhs=Pm, start=True, stop=True),
             reads=(T_Pm, T_const), writes=(bankT[7],))
        for h in range(4):
            for m in range(2):
                cs_ = 16 * (2 * h + m)
                P.op("pe", I("matmul", banks[2][:, cs_:cs_ + 16], lhsT=vtk[:, 0, 128 * h:128 * h + 128],
                             rhs=Pm[:, cs_:cs_ + 16], start=True, stop=True), reads=(T_Pm, T_v[0]),
                     writes=(bankT[2],))
        e0, e1, e2, e3, e4 = [t[:, 0:128] for t in ep]
        P.op("act", I("activation", e0, banks[7][:, 0:128], AF.Ln), reads=(bankT[7],), writes=(T_ep[0],))
        P.op("act", I("activation", e0, e0, AF.Exp, scale=-1.0), reads=(T_ep[0],), writes=(T_ep[0],))
        P.op("dve", I("tensor_tensor", e1, banks[2][:, 0:128], e0, ALU.mult), reads=(bankT[2], T_ep[0]),
             writes=(T_ep[1],))
        e1v = e1.rearrange("p (h m q) -> p h m q", h=4, m=2)
        e3v = ep[3][:, 0:64].rearrange("p (h q) -> p h q", h=4)
        e4v = ep[4][:, 0:64].rearrange("p (h q) -> p h q", h=4)
        P.op("dve", I("scalar_tensor_tensor", e3v, e1v[:, :, 1, :], nlamc, e1v[:, :, 0, :], ALU.mult, ALU.add),
             reads=(T_ep[1], T_const), writes=(T_ep[3],))
        P.op("dve", I("tensor_tensor", sqb[:, 0:64], ep[3][:, 0:64], ep[3][:, 0:64], ALU.mult), reads=(T_ep[3],),
             writes=(T_sq,))
        P.op("pe", I("matmul", banks[7][:, 0:64], lhsT=onesb, rhs=sqb[:, 0:64], start=True, stop=True),
             reads=(T_sq, T_const), writes=(bankT[7],))
        P.op("act", I("activation", ep[4][:, 0:64], banks[7][:, 0:64], AF.Ln, bias=EPS, scale=1.0 / 128),
             reads=(bankT[7],), writes=(T_ep[4],))
        P.op("act", I("activation", ep[4][:, 0:64], ep[4][:, 0:64], AF.Exp, scale=-0.5), reads=(T_ep[4],),
             writes=(T_ep[4],))
        P.op("dve", I("scalar_tensor_tensor", bufM[:, 0:4, 0:16], e3v, gdc, e4v, ALU.mult, ALU.mult),
             reads=(T_ep[3], T_ep[4], T_const), writes=[T_mg[h][0] for h in range(4)])
        _D = _rec
        _rec = []
        P.op = lambda *a_, **k_: _rec.append((a_, k_))
        def cg_(g):
            return 16 * (4 * (g % 2) + g // 2)

        def zt_m(bank, g, stop):
            return [I("matmul", bank[:, cg_(g):cg_(g) + 16], lhsT=qkT[:, 12 + g // 2, 0:128],
                      rhs=QPm[:, 8 + 4 * (g % 2) + g // 2, :], start=True, stop=False),
                    I("matmul", bank[:, cg_(g):cg_(g) + 16], lhsT=identb, rhs=maskSp[:, 0:16], start=False,
                      stop=stop)]
        for g in range(8):
            P.op("pe", zt_m(banks[4], g, True), reads=(T_qk[12 + g // 2][0], T_m, T_const), writes=(bankT[4],))
        T_Em = Trk()
        T_SPm = Trk()
        T_Am = Trk()
        P.op("act", I("activation", Em, banks[4][:, 0:128], AF.Exp, scale=-1.0, bias=metab),
             reads=(bankT[4], T_const), writes=(T_Em,))
        P.op("act", I("activation", SPm, Em, AF.Ln, bias=1.0, scale=1.0), reads=(T_Em,), writes=(T_SPm,))
        for g in range(8):
            P.op("pe", zt_m(banks[1], g, False) +
                 [I("matmul", banks[1][:, cg_(g):cg_(g) + 16], lhsT=trib, rhs=SPm[:, cg_(g):cg_(g) + 16],
                    start=False, stop=True)],
                 reads=(T_qk[12 + g // 2][0], T_m, T_SPm, T_const), writes=(bankT[1],))
        P.op("act", I("activation", Am, banks[1][:, 0:128], AF.Exp, scale=-1.0, bias=metab),
             reads=(bankT[1], T_const), writes=(T_Am,))
        for g in range(8):
            vc = 512 + 128 * (g // 2)
            P.op("pe", I("matmul", banks[3][:, cg_(g):cg_(g) + 16], lhsT=vtk[:, 0, vc:vc + 128],
                         rhs=Am[:, cg_(g):cg_(g) + 16], start=True, stop=True), reads=(T_Am, T_v[0]),
                 writes=(bankT[3],))
        for p_ in range(2):
            rr = slice(64 * p_, 64 * p_ + 64)
            cc_ = slice(64 * p_, 64 * p_ + 64)
            ce_ = slice(128 + 64 * p_, 128 + 64 * p_ + 64)
            P.op("dve", I("tensor_copy", ep[1][rr, ce_], banks[3][rr, cc_]), reads=(bankT[3],), writes=(T_epm[1],))
            P.op("dve", I("tensor_tensor", sqb[rr, ce_], ep[1][rr, ce_], ep[1][rr, ce_], ALU.mult),
                 reads=(T_epm[1],), writes=(T_sqm,))
            P.op("pe", I("matmul", banks[5][:, cc_], lhsT=onesb[rr, :], rhs=sqb[rr, ce_], start=True, stop=True),
                 reads=(T_sqm, T_const), writes=(bankT[5],))
            P.op("act", I("activation", ep[4][rr, ce_], banks[5][rr, cc_], AF.Ln, bias=EPS, scale=1.0 / 64),
                 reads=(bankT[5],), writes=(T_epm[4],))
            P.op("act", I("activation", ep[4][rr, ce_], ep[4][rr, ce_], AF.Exp, scale=-0.5), reads=(T_epm[4],),
                 writes=(T_epm[4],))
            P.op("dve", I("scalar_tensor_tensor", bufM[rr, 4:8, 0:16],
                          ep[1][rr, ce_].rearrange("p (h q) -> p h q", h=4), gsbc[rr],
                          ep[4][rr, ce_].rearrange("p (h q) -> p h q", h=4), ALU.mult, ALU.mult),
                 reads=(T_epm[1], T_epm[4], T_const), writes=[T_mg[4 + c][0] for c in range(4)])

        _S = _rec
        P.op = _orig_op
        for i_ in range(max(len(_D), len(_S))):
            if i_ < len(_D):
                P.op(*_D[i_][0], **_D[i_][1])
            if i_ < len(_S):
                P.op(*_S[i_][0], **_S[i_][1])
        P.op("dve", I("memset", sqb[0:1, 511:512], 0.0), reads=(T_epm[1], T_epm[4], T_sqm),
             writes=(T_ep[1], T_ep[4], T_sq))
        tasks = [("d", h, qt) for qt in range(NT, 0, -1) for h in range(4)] + \
                [("s", g, qt) for qt in range(NT, 0, -1) for g in range(8)]
        pending = []
        ctr = {"S": 0, "P": 0, "Z": 0, "E": 0, "SP": 0, "A": 0, "R": 0, "CS": 0}

        def run_pending(k):
            for _ in range(k):
                if pending:
                    pending.pop(0)()

        def prep(ti):
            kind, hh, qt = tasks[ti]
            q0, nq = qtile(qt)
            st_ = ti % 2
            if kind == "d":
                for m in range(2):
                    P.op("pool", I("tensor_copy", QP[st_][m][64 * m:64 * m + 64, 0:nq],
                                   qkT[64 * m:64 * m + 64, hh, q0:q0 + nq]), reads=(T_qk[hh][qt],),
                         writes=(T_QP[st_][m],))
            else:
                r0 = 64 * (hh % 2)
                P.op("dve", I("tensor_copy", QP[st_][hh % 2][r0:r0 + 64, 0:nq],
                              qkT[r0:r0 + 64, 8 + hh // 2, q0:q0 + nq]), reads=(T_qk[8 + hh // 2][qt],),
                     writes=(T_QP[st_][hh % 2],))

        ctx = []
        for ti, (kind, hh, qt) in enumerate(tasks):
            q0, nq = qtile(qt)
            kbs = keyblocks(qt)
            if kind == "d":
                if qt > 0:
                    kbs = kbs[1:] + [0]
            else:
                kbs = kbs[::-1]
            ctx.append(dict(ti=ti, kind=kind, hh=hh, qt=qt, q0=q0, nq=nq, kbs=kbs, nkb=len(kbs), st=ti % 2))
        n_d = 4 * NT
        d_items = [(ti, i) for ti in range(n_d) for i in range(ctx[ti]["nkb"])]
        s_items = [(ti, i) for ti in range(n_d, len(tasks)) for i in range(ctx[ti]["nkb"])]

        sb_of = {}

        def d_S(item, m):
            ti, i = item
            c = ctx[ti]
            h, qt, nq, st_ = c["hh"], c["qt"], c["nq"], c["st"]
            kb = c["kbs"][i]
            _, nk, kp, c0, diag, mw, bi = geom(qt, kb)
            sbk = [0, 1, 2][ctr["S"] % 3]
            ctr["S"] += 1
            ins = [I("matmul", banks[sbk][0:nk, c0:nq], lhsT=qkT[:, 4 + h, kp:kp + nk],
                     rhs=QP[st_][m][:, c0:nq], start=True, stop=(not diag and h != 0))]
            if h == 0:
                ins.append(I("matmul", banks[sbk][0:nk, c0:nq], lhsT=onesr0[:, 0:nk],
                             rhs=qb8p[:, c0:nq], start=False, stop=not diag))
            if diag:
                ins.append(I("matmul", banks[sbk][0:nk, c0:c0 + mw], lhsT=identb[0:nk, 0:nk],
                             rhs=maskD[0:nk, 0:mw], start=False, stop=True))
            P.op("pe", ins, reads=(T_qk[4 + h][tile_of_block(kb)], T_QP[st_][m], T_const), writes=(bankT[sbk],))
            sb_of[(item, m)] = sbk

        def d_epi(c):
            h, qt, q0, nq, st_ = c["hh"], c["qt"], c["q0"], c["nq"], c["st"]
            OBK = [3, 4] if st_ == 0 else [5, 6]
            O1, O2 = [banks[a][:, 0:nq] for a in OBK]
            S7 = banks[7][:, 0:nq]
            e0, e1, e2, e3, e4 = [t[:, 0:nq] for t in ep]

            def t0():
                P.op("pe", I("matmul", S7, lhsT=onesf, rhs=Pacc[st_][0][:, 0:nq], start=True, stop=True),
                     reads=(T_Pacc[st_][0], T_const), writes=(bankT[7],))

            def t1():
                P.op("act", I("activation", e0, S7, AF.Ln), reads=(bankT[7],), writes=(T_ep[0],))

            def t2():
                P.op("pe", I("matmul", S7, lhsT=onesf, rhs=Pacc[st_][1][:, 0:nq], start=True, stop=True),
                     reads=(T_Pacc[st_][1], T_const), writes=(bankT[7],))

            def t3():
                P.op("act", I("activation", e2, S7, AF.Ln), reads=(bankT[7],), writes=(T_ep[2],))
                P.op("act", I("activation", e0, e0, AF.Exp, scale=-1.0), reads=(T_ep[0],), writes=(T_ep[0],))

            def t4():
                P.op("act", I("activation", e2, e2, AF.Exp, scale=-1.0), reads=(T_ep[2],), writes=(T_ep[2],))
                P.op("dve", I("tensor_tensor", e1, O1, e0, ALU.mult), reads=(bankT[OBK[0]], T_ep[0]),
                     writes=(T_ep[1],))

            def t5():
                P.op("dve", I("tensor_tensor", e3, O2, e2, ALU.mult), reads=(bankT[OBK[1]], T_ep[2]),
                     writes=(T_ep[3],))

            def t6():
                P.op("dve", I("scalar_tensor_tensor", e1, e3, nlamc, e1, ALU.mult, ALU.add),
                     reads=(T_ep[3], T_ep[1], T_const), writes=(T_ep[1],))

            def t7():
                P.op("pool", I("tensor_tensor", sqb[:, 0:nq], e1, e1, ALU.mult), reads=(T_ep[1],), writes=(T_sq,))

            def t8():
                P.op("pe", I("matmul", S7, lhsT=onesb, rhs=sqb[:, 0:nq], start=True, stop=True),
                     reads=(T_sq, T_const), writes=(bankT[7],))

            def t9():
                P.op("act", I("activation", e4, S7, AF.Ln, bias=EPS, scale=1.0 / 128), reads=(bankT[7],),
                     writes=(T_ep[4],))

            def t10():
                P.op("act", I("activation", e4, e4, AF.Exp, scale=-0.5), reads=(T_ep[4],), writes=(T_ep[4],))

            def t11():
                P.op("dve", I("scalar_tensor_tensor", bufM[:, h, q0:q0 + nq], e1, gdc, e4, ALU.mult, ALU.mult),
                     reads=(T_ep[1], T_ep[4], T_const), writes=(T_mg[h][qt],))
            return [t0, t1, t2, t3, t4, t5, t6, t7, t8, t9, t10, t11]

        prep(0)
        d_S(d_items[0], 0)
        d_S(d_items[0], 1)
        for j, item in enumerate(d_items):
            ti, i = item
            c = ctx[ti]
            h, qt, nq, st_ = c["hh"], c["qt"], c["nq"], c["st"]
            OBK = [3, 4] if st_ == 0 else [5, 6]
            if i == 0 and ti + 1 < len(tasks):
                prep(ti + 1)
            kb = c["kbs"][i]
            _, nk, kp, c0, diag, mw, bi = geom(qt, kb)
            nxt = d_items[j + 1] if j + 1 < len(d_items) else None
            pis = []
            for m in range(2):
                sbk = sb_of[(item, m)]
                pi = ctr["P"] % 4
                ctr["P"] += 1
                pis.append(pi)
                P.op("act", I("activation", Pt[pi][0:nk, c0:nq], banks[sbk][0:nk, c0:nq], AF.Exp,
                              bias=biasT[0:nk, h, bi:bi + 1], scale=0.125),
                     reads=(bankT[sbk], T_const), writes=(T_P[pi],))
                if nxt is not None:
                    d_S(nxt, m)
            for m in range(2):
                pi = pis[m]
                eng_ = "dve" if m == 0 else ACC2_ENG
                if i == 0:
                    assert nk == 128 and c0 == 0
                    P.op(eng_, I("tensor_copy", Pacc[st_][m][:, 0:nq], Pt[pi][:, 0:nq]), reads=(T_P[pi],),
                         writes=(T_Pacc[st_][m],))
                else:
                    P.op(eng_, I("tensor_tensor", Pacc[st_][m][0:nk, c0:nq], Pacc[st_][m][0:nk, c0:nq],
                                 Pt[pi][0:nk, c0:nq], ALU.add), reads=(T_P[pi], T_Pacc[st_][m]),
                         writes=(T_Pacc[st_][m],))
            for m in range(2):
                pi = pis[m]
                P.op("pe", I("matmul", banks[OBK[m]][:, c0:nq], lhsT=vtk[0:nk, kb, 128 * h:128 * h + 128],
                             rhs=Pt[pi][0:nk, c0:nq], start=(i == 0), stop=(i == c["nkb"] - 1)),
                     reads=(T_P[pi], T_v[kb]), writes=(bankT[OBK[m]],))
            run_pending(1)
            if i == c["nkb"] - 1:
                run_pending(len(pending))
                pending.extend(d_epi(c))
        run_pending(len(pending))

        zb_of = {}
        hist = {}

        def s_zterm(bank, item, start, stop):
            ti, i = item
            c = ctx[ti]
            g, qt, nq, st_ = c["hh"], c["qt"], c["nq"], c["st"]
            kb = c["kbs"][i]
            _, nk, kp, c0, diag, mw, bi = geom(qt, kb)
            ck = 12 + g // 2
            par = g % 2
            ins = [I("matmul", bank[0:nk, c0:nq], lhsT=qkT[:, ck, kp:kp + nk],
                     rhs=QP[st_][par][:, c0:nq], start=start, stop=(stop and not diag))]
            if diag:
                ins.append(I("matmul", bank[0:nk, c0:c0 + mw], lhsT=identb[0:nk, 0:nk],
                             rhs=maskSp[0:nk, 0:mw], start=False, stop=stop))
            return ins, (T_qk[ck][tile_of_block(kb)], T_QP[st_][par], T_const)

        def s_Z(item):
            zb = ctr["Z"] % 2
            ctr["Z"] += 1
            ins, rd = s_zterm(banks[zb], item, True, True)
            P.op("pe", ins, reads=rd, writes=(bankT[zb],))
            zb_of[item] = zb

        def s_epi(c):
            g, qt, q0, nq, st_ = c["hh"], c["qt"], c["q0"], c["nq"], c["st"]
            OB = 5 + st_
            r0 = 64 * (g % 2)
            rr = slice(r0, r0 + 64)
            e1 = ep[1][rr, 0:nq]
            e4 = ep[4][rr, 0:nq]

            def t0():
                P.op("dve", I("tensor_copy", e1, banks[OB][rr, 0:nq]), reads=(bankT[OB],), writes=(T_ep[1],))

            def t1():
                P.op("pool", I("tensor_tensor", sqb[rr, 0:nq], e1, e1, ALU.mult), reads=(T_ep[1],), writes=(T_sq,))

            def t2():
                P.op("pe", I("matmul", banks[7][:, 0:nq], lhsT=onesb[rr, :], rhs=sqb[rr, 0:nq], start=True,
                             stop=True), reads=(T_sq, T_const), writes=(bankT[7],))

            def t3():
                P.op("act", I("activation", e4, banks[7][rr, 0:nq], AF.Ln, bias=EPS, scale=1.0 / 64),
                     reads=(bankT[7],), writes=(T_ep[4],))

            def t4():
                P.op("act", I("activation", e4, e4, AF.Exp, scale=-0.5), reads=(T_ep[4],), writes=(T_ep[4],))

            def t5():
                P.op("dve", I("scalar_tensor_tensor", bufM[rr, 4 + g // 2, q0:q0 + nq], e1, gsbc[rr], e4,
                              ALU.mult, ALU.mult), reads=(T_ep[1], T_ep[4], T_const),
                     writes=(T_mg[4 + g // 2][qt],))
            return [t0, t1, t2, t3, t4, t5]

        def s_stage2(item):
            ti, i = item
            c = ctx[ti]
            g, nq, st_ = c["hh"], c["nq"], c["st"]
            OB = 5 + st_
            vc = 512 + 128 * (g // 2)
            (kb2, nk2, c02, csb2) = hist[item]
            ai = ctr["A"] % 3
            ctr["A"] += 1
            bkw2 = dict(bias=metab) if kb2 == 0 else {}
            P.op("act", I("activation", At[ai][0:nk2, c02:nq], banks[csb2][0:nk2, c02:nq], AF.Exp,
                          scale=-1.0, **bkw2), reads=(bankT[csb2], T_const), writes=(T_A[ai],))
            P.op("pe", I("matmul", banks[OB][:, c02:nq], lhsT=vtk[0:nk2, kb2, vc:vc + 128],
                         rhs=At[ai][0:nk2, c02:nq], start=False, stop=(i == c["nkb"] - 1)),
                 reads=(T_A[ai], T_v[kb2]), writes=(bankT[OB],))
            if i == c["nkb"] - 1:
                run_pending(len(pending))
                pending.extend(s_epi(c))

        s_Z(s_items[0])
        for j in range(len(s_items) + 2):
            if j < len(s_items) and s_items[j][1] == 0 and s_items[j][0] + 1 < len(tasks):
                prep(s_items[j][0] + 1)
            if j + 1 < len(s_items):
                s_Z(s_items[j + 1])
            if j < len(s_items):
                item = s_items[j]
                ti, i = item
                c = ctx[ti]
                g, qt, nq, st_ = c["hh"], c["qt"], c["nq"], c["st"]
                kb = c["kbs"][i]
                _, nk, kp, c0, diag, mw, bi = geom(qt, kb)
                if i == 0:
                    P.op("dve", I("memset", RS32, 0.0), writes=(T_RS32,))
                zb = zb_of[item]
                ei = ctr["E"] % 2
                ctr["E"] += 1
                bkw = dict(bias=metab) if kb == 0 else {}
                P.op("act", I("activation", Et[ei][0:nk, c0:nq], banks[zb][0:nk, c0:nq], AF.Exp, scale=-1.0,
                              **bkw), reads=(bankT[zb], T_const), writes=(T_E[ei],))
            if j >= 2:
                it2 = s_items[j - 2]
                if it2[1] == 0:
                    c2 = ctx[it2[0]]
                    P.op("pe", I("matmul", banks[5 + c2["st"]][:, 0:c2["nq"]], lhsT=zerosb,
                                 rhs=qkT[:, 0, 0:c2["nq"]], start=True, stop=False),
                         reads=(T_const, T_qk[0][0]), writes=(bankT[5 + c2["st"]],))
                s_stage2(it2)
            if j < len(s_items):
                si = ctr["SP"] % 3
                ctr["SP"] += 1
                P.op("act", I("activation", SPt[si][0:nk, c0:nq], Et[ei][0:nk, c0:nq], AF.Ln, bias=1.0,
                              scale=1.0), reads=(T_E[ei],), writes=(T_SP[si],))
                csb = [2, 3, 4][ctr["CS"] % 3]
                ctr["CS"] += 1
                ins, rd = s_zterm(banks[csb], item, True, False)
                P.op("pe", ins, reads=rd, writes=(bankT[csb],))
                rb = (ctr["R"] - 1) % 2
                ins = [I("matmul", banks[csb][0:nk, c0:nq], lhsT=trib[0:nk, 0:nk], rhs=SPt[si][0:nk, c0:nq],
                         start=False, stop=(i == 0))]
                rd = [T_SP[si], T_const]
                if i > 0:
                    ins.append(I("matmul", banks[csb][0:nk, c0:nq], lhsT=onesb[:, 0:nk],
                                 rhs=RSb[rb][:, c0:nq], start=False, stop=True))
                    rd.append(T_RSb[rb])
                P.op("pe", ins, reads=rd, writes=(bankT[csb],))
                if i < c["nkb"] - 1:
                    c0n = geom(qt, c["kbs"][i + 1])[3]
                    P.op("dve", I("tensor_tensor", RS32[0:nk, c0:nq], RS32[0:nk, c0:nq],
                                  SPt[si][0:nk, c0:nq], ALU.add), reads=(T_SP[si], T_RS32), writes=(T_RS32,))
                    rbn = ctr["R"] % 2
                    ctr["R"] += 1
                    P.op("dve", I("tensor_copy", RSb[rbn][:, c0n:nq], RS32[:, c0n:nq]),
                         reads=(T_RS32,), writes=(T_RSb[rbn],))
                hist[item] = (kb, nk, c0, csb)
            run_pending(1)
        run_pending(len(pending))
        P.barrier()
        if dbg and s == 0:
            dsd = DSem(P)
            P.op("sp", I("dma_start", out=dbg_d["mg"], in_=flat(O_M, 8 * L // 2, BF16)), ds=dsd)
            P.barrier()

        oE = O_E
        xin = [flat(oE + 1024 * i, 1024) for i in range(3)]
        T_xin = [Trk() for _ in range(3)]
        xs2 = [flat(oE + 3072 + 512 * i, 512, BF16) for i in range(2)]
        T_xs2 = [Trk() for _ in range(2)]
        junk = flat(oE + 4096, 512, BF16)
        T_junk = Trk()
        g_bc = flat(oE + 4608, 1024)
        T_g = Trk()
        ssb = flat(oE + 5632, 64)
        T_ss = [Trk() for _ in range(4)]
        T_rs = [Trk() for _ in range(4)]
        load_bc(g_bc, g_ffn_d, T_g)
        T_wo = [Trk() for _ in range(4)]
        for sl in range(4):
            P.op("pool", I("dma_start", out=wout[:, :, 256 * sl:256 * sl + 256],
                           in_=w_out_v[:, :, 256 * sl:256 * sl + 256]), writes=(T_wo[sl],), ds=sem_w[sl])
        T_h1 = [Trk() for _ in range(NB)]
        T_hn = [Trk() for _ in range(NB)]
        pb = 0
        ob_of = {}

        def c1_mm(tb):
            nonlocal pb
            p0, n = blk(tb)
            load_x(s, tb, xin[tb % 3], T_xin[tb % 3], sem_x[tb % 3])
            qtile_i = tile_of_block(tb)
            lst = []
            for half in range(2):
                b = pb % 4
                pb += 1
                P.op("pe", [I("matmul", banks[b][0:n, :], lhsT=bufM[:, cc, p0:p0 + n],
                              rhs=wout[:, cc, 512 * half:512 * half + 512], start=(cc == 0), stop=(cc == 7))
                            for cc in range(8)],
                     reads=[T_mg[c][qtile_i] for c in range(8)] + [T_wo[2 * half], T_wo[2 * half + 1]],
                     writes=(bankT[b],))
                lst.append(b)
            ob_of[tb] = lst

        c1_mm(0)
        prev_fin = None
        for tb in range(NB):
            p0, n = blk(tb)
            i3, i2, k4 = tb % 3, tb % 2, tb % 4
            for half in range(2):
                b = ob_of[tb][half]
                P.op("dve", I("tensor_tensor", h1[0:n, tb, 512 * half:512 * half + 512], banks[b][0:n, :],
                              xin[i3][0:n, 512 * half:512 * half + 512], ALU.add),
                     reads=(bankT[b], T_xin[i3]), writes=(T_h1[tb],))
            if tb + 1 < NB:
                c1_mm(tb + 1)
            fin = norm_transpose(h1[0:n, tb, :], T_h1[tb], n, tb, g_bc, T_g, xs2[i2], T_xs2[i2],
                                 ssb[:, k4:k4 + 1], T_ss[k4], ssb[:, 8 + k4:9 + k4], T_rs[k4], junk, T_junk,
                                 6 + i2, bufX, T_hn[tb])
            if prev_fin is not None:
                prev_fin()
            prev_fin = fin
        prev_fin()
        P.barrier()
        if dbg and s == 0:
            dsd = DSem(P)
            P.op("sp", I("dma_start", out=dbg_d["h1"], in_=flat(O_R, NB * 1024)), ds=dsd)
            P.barrier()

        wup = [[flat(O_M + 4096 * i + 2048 * j, 2048, BF16).rearrange("p (c f) -> p c f", c=8) for j in range(2)]
               for i in range(2)]
        T_wup = [[[Trk(), Trk()] for j in range(2)] for i in range(2)]
        wd = [wd0, flat(oE + 4096, 2048, BF16).rearrange("p (c f) -> p c f", c=4)]
        T_wd = [Trk(), Trk()]
        yt = [[flat(oE + 1536 * i + 512 * j, 512) for j in range(3)] for i in range(2)]
        T_y = [[Trk() for j in range(3)] for i in range(2)]
        g_bc = flat(oE + 3072, 1024)
        T_g = Trk()
        load_bc(g_bc, g_fin_d, T_g)
        ssb = flat(O_RJ, 64)
        T_ss = [Trk() for _ in range(4)]
        T_rs = [Trk() for _ in range(4)]
        junk = flat(O_RJ + 64, 512, BF16)
        T_junk = Trk()
        T_a = [[Trk() for _ in range(NB)] for _ in range(4)]
        groups = [(4 * i, min(4, 22 - 4 * i)) for i in range(6)]
        ftiles = []
        p = 0
        while p < L:
            n = min(510, L - p)
            ftiles.append((p, n, 0 if p == 0 else 2))
            p += n
        pb = 0
        yc = 0
        for gi, (fc0, nch) in enumerate(groups):
            wb = gi % 2
            for j in range(2):
                for sl in range(nch // 2):
                    col = (0 if j == 0 else DFF) + 128 * fc0 + 256 * sl
                    P.op("pool", I("dma_start", out=wup[wb][j][:, :, 256 * sl:256 * sl + 256],
                                   in_=w_up_v[:, :, col:col + 256]), writes=(T_wup[wb][j][sl],), ds=sem_w[2 * j + sl])
            P.op("pool", I("dma_start", out=wd[wb][:, 0:nch, :], in_=w_dn_v[:, fc0:fc0 + nch, :]),
                 writes=(T_wd[wb],), ds=sem_w[wb])
            for jc in range(nch):
                c = fc0 + jc
                for (p0, n, ho) in ftiles:
                    hb = []
                    tbs = blocks_of(p0 - ho, n + ho)
                    for j in range(2):
                        b = pb % 4
                        pb += 1
                        hb.append(b)
                        P.op("pe", [I("matmul", banks[b][:, 0:n + ho], lhsT=wup[wb][j][:, cc, 128 * jc:128 * jc + 128],
                                      rhs=bufX[:, cc, p0 - ho:p0 + n], start=(cc == 0), stop=(cc == 7))
                                    for cc in range(8)],
                             reads=[T_wup[wb][j][jc // 2]] + [T_hn[tb] for tb in tbs], writes=(bankT[b],))
                    ys = yc % 2
                    yc += 1
                    for j in range(2):
                        b = hb[j]
                        cc_ = c + 22 * j
                        y = yt[ys][j]
                        P.op("act", I("activation", y[:, 0:n], banks[b][:, ho:ho + n], AF.Identity,
                                      bias=cw[:, 3, cc_:cc_ + 1], scale=cw[:, 2, cc_:cc_ + 1]),
                             reads=(bankT[b], T_const), writes=(T_y[ys][j],))
                        j1 = 0 if ho == 2 else 1
                        P.op("dve", I("scalar_tensor_tensor", y[:, j1:n], banks[b][:, ho + j1 - 1:ho + n - 1],
                                      cw[:, 1, cc_:cc_ + 1], y[:, j1:n], ALU.mult, ALU.add),
                             reads=(bankT[b], T_y[ys][j], T_const), writes=(T_y[ys][j],))
                        j2 = 0 if ho == 2 else 2
                        P.op("dve", I("scalar_tensor_tensor", y[:, j2:n], banks[b][:, ho + j2 - 2:ho + n - 2],
                                      cw[:, 0, cc_:cc_ + 1], y[:, j2:n], ALU.mult, ALU.add),
                             reads=(bankT[b], T_y[ys][j], T_const), writes=(T_y[ys][j],))
                    P.op("act", I("activation", yt[ys][2][:, 0:n], yt[ys][0][:, 0:n], AF.Silu),
                         reads=(T_y[ys][0],), writes=(T_y[ys][2],))
                    tbs = blocks_of(p0, n)
                    P.op("pool", I("tensor_tensor", a_g[:, jc, p0:p0 + n], yt[ys][2][:, 0:n], yt[ys][1][:, 0:n],
                                   ALU.mult), reads=(T_y[ys][2], T_y[ys][1]), writes=[T_a[jc][tb] for tb in tbs])
            last = gi == len(groups) - 1
            for tb in range(1, NB):
                p0, n = blk(tb)
                k4 = tb % 4
                for half in range(2):
                    b = 4 + pb % 2
                    pb += 1
                    P.op("pe", [I("matmul", banks[b][0:n, :], lhsT=a_g[:, jc, p0:p0 + n],
                                  rhs=wd[wb][:, jc, 512 * half:512 * half + 512], start=(jc == 0),
                                  stop=(jc == nch - 1)) for jc in range(nch)],
                         reads=[T_a[jc][tb] for jc in range(nch)] + [T_wd[wb]], writes=(bankT[b],))
                    P.op("dve", I("tensor_tensor", h1[0:n, tb, 512 * half:512 * half + 512], banks[b][0:n, :],
                                  h1[0:n, tb, 512 * half:512 * half + 512], ALU.add),
                         reads=(bankT[b], T_h1[tb]), writes=(T_h1[tb],))
                if last:
                    ss = ssb[:, k4:k4 + 1]
                    rs = ssb[:, 8 + k4:9 + k4]
                    P.op("act", I("activation", junk[0:n], h1[0:n, tb, :], AF.Square, accum_out=ss[0:n]),
                         reads=(T_h1[tb],), writes=(T_junk, T_ss[k4]))
                    rstd_ops(ss, rs, n, T_ss[k4], T_rs[k4], 1.0 / D)
                    P.op("dve", I("scalar_tensor_tensor", h1[0:n, tb, :], h1[0:n, tb, :], rs[0:n], g_bc[0:n],
                                  ALU.mult, ALU.mult), reads=(T_h1[tb], T_rs[k4], T_g), writes=(T_h1[tb],))
                    P.op("sp", I("dma_start", out=out_d[s, 128 * (tb - 1):128 * tb, :], in_=h1[0:128, tb, :]),
                         reads=(T_h1[tb],), ds=sem_o)
        P.barrier()

    P.emit(nc)
    st.close()
    return nc


_CACHE = {}


def prep_inputs(inputs):
    f = lambda a: np.ascontiguousarray(np.asarray(a, dtype=np.float32))
    shared = {
        "meta": f(inputs["meta_tokens"]),
        "w_in": f(inputs["w_in"][0]),
        "w_out": f(inputs["w_out"][0]),
        "w_up": f(inputs["w_up"][0]),
        "w_down": f(inputs["w_down"][0]),
        "convw": f(np.concatenate([np.asarray(inputs["conv_w"][0]), np.asarray(inputs["conv_b"])], axis=0)),
        "g_attn": f(inputs["g_attn"]),
        "g_ffn": f(inputs["g_ffn"]),
        "g_fin": f(np.asarray(inputs["g_final"]).reshape(1, D)),
        "g_diff": f(inputs["g_diff"]),
        "g_sb": f(inputs["g_sb"]),
        "lam": f(np.concatenate([np.asarray(inputs[k]) for k in ("lam_q1", "lam_k1", "lam_q2", "lam_k2")], axis=0)),
    }
    return shared


def kernel(**inputs):
    x = np.asarray(inputs["x"], dtype=np.float32)
    B, SEQ, _ = x.shape
    ncores = 8
    nseq = B // ncores
    key = (nseq, SEQ)
    if key not in _CACHE:
        _CACHE[key] = build(nseq, SEQ)
    nc = _CACHE[key]
    shared = prep_inputs(inputs)
    in_maps = []
    for c in range(ncores):
        m = dict(shared)
        m["x"] = np.ascontiguousarray(x[c * nseq:(c + 1) * nseq])
        in_maps.append(m)
    res = run_bass_kernel_spmd(nc, in_maps, core_ids=list(range(ncores)))
    return np.concatenate([np.asarray(r["out"]) for r in res.results], axis=0).astype(np.float32)
```
